# Optimizing a Trainium2 kernel written in Bass

```python
import jax, jax.numpy as jnp
from jax import lax
import numpy as np

D_MODEL = 1024
BATCH = 16
SEQ = 2048
DEPTH = 2
DEC_BATCH = 32
DEC_SEQ = 2048
PAST_LEN = 128

N_MIXERS = 2
N_FNET_LAYERS = (DEPTH + 1) // 2
N_LRU_LAYERS = DEPTH // 2
FNET_GROUPS = 4
FNET_GROUP_WIDTH = D_MODEL // FNET_GROUPS
D_RNN = 1280
LRU_BLOCKS = 10
LRU_BLOCK = D_RNN // LRU_BLOCKS
LRU_C = 8.0
LRU_CONV = 4
LRU_PAD_L, LRU_PAD_R = 2, 1
N_MEM = 256
XA_HEADS = 4
XA_HEAD_DIM = D_MODEL // XA_HEADS
D_FF = 2816
FFN_CONV = 3
EPS = 1e-6

kernel_name = "hybrid_fnet_rglru_xattn_encoder"


def _rmsnorm(x, g):
    xf = x.astype(jnp.float32)
    y = xf * lax.rsqrt(jnp.mean(xf * xf, axis=-1, keepdims=True) + EPS)
    return (y * g.astype(jnp.float32)).astype(x.dtype)


def _dwconv(x, w, b, pad_l, pad_r):
    s = x.shape[1]
    xp = jnp.pad(x, ((0, 0), (pad_l, pad_r), (0, 0)))
    out = xp[:, 0:s] * w[0]
    for k in range(1, w.shape[0]):
        out = out + xp[:, k:k + s] * w[k]
    return out + b


def _fourier_mixer(xn, w_out, b_out):
    bsz, s, d = xn.shape
    xg = xn.reshape(bsz, s, FNET_GROUPS, FNET_GROUP_WIDTH).astype(jnp.float32)
    f = jnp.fft.fft2(xg, axes=(1, 3), norm="ortho").real
    f = f.reshape(bsz, s, d).astype(xn.dtype)
    return f @ w_out + b_out


def _lin_combine(c1, c2):
    a1, b1 = c1
    a2, b2 = c2
    return a1 * a2, a2 * b1 + b2


def _rglru_mixer(xn, w_in, conv_w, conv_b, w_a, b_a, w_i, b_i, lam, w_out):
    bsz, s, _ = xn.shape
    u = xn @ w_in
    gate_br, rec_br = jnp.split(u, 2, axis=-1)
    gate = jax.nn.gelu(gate_br)
    c = _dwconv(rec_br, conv_w, conv_b, LRU_PAD_L, LRU_PAD_R)
    cb = c.reshape(bsz, s, LRU_BLOCKS, LRU_BLOCK)
    r = jax.nn.sigmoid((jnp.einsum('bshi,ehij->ebshj', cb, w_a).reshape(2, bsz, s, D_RNN)
                        + b_a[:, None, None, :]).astype(jnp.float32))
    ig = jax.nn.sigmoid((jnp.einsum('bshi,ehij->ebshj', cb, w_i).reshape(2, bsz, s, D_RNN)
                         + b_i[:, None, None, :]).astype(jnp.float32))
    log_a = -LRU_C * jax.nn.softplus(-lam.astype(jnp.float32))[:, None, None, :] * r
    a = jnp.exp(log_a)
    bterm = jnp.sqrt(-jnp.expm1(2.0 * log_a)) * (ig * c.astype(jnp.float32)[None])
    _, h_fwd = lax.associative_scan(_lin_combine, (a[0], bterm[0]), axis=1)
    _, h_bwd = lax.associative_scan(_lin_combine, (a[1], bterm[1]), axis=1, reverse=True)
    h = (h_fwd + h_bwd).astype(xn.dtype)
    return (h * gate) @ w_out


def _cross_attention(xn, mn, w_q, w_kv, w_o):
    bsz, s, d = xn.shape
    m = mn.shape[1]
    q = (xn @ w_q).reshape(bsz, s, XA_HEADS, XA_HEAD_DIM)
    k, v = jnp.split(mn @ w_kv, 2, axis=-1)
    k = k.reshape(bsz, m, XA_HEADS, XA_HEAD_DIM)
    v = v.reshape(bsz, m, XA_HEADS, XA_HEAD_DIM)
    scores = jnp.einsum('bshk,bmhk->bhsm', q, k).astype(jnp.float32) * (XA_HEAD_DIM ** -0.5)
    p = jax.nn.softmax(scores, axis=-1).astype(v.dtype)
    o = jnp.einsum('bhsm,bmhk->bshk', p, v).reshape(bsz, s, d)
    return o @ w_o


def _conv_ffn(xn, w_up, conv_w, conv_b, w_down):
    g, v = jnp.split(xn @ w_up, 2, axis=-1)
    g = _dwconv(g, conv_w, conv_b, FFN_CONV // 2, FFN_CONV // 2)
    return (jax.nn.gelu(g) * v) @ w_down


def _trunk(x, mem, p):
    for i in range(DEPTH):
        xn = _rmsnorm(x, p['norm_mix'][i])
        if i % N_MIXERS == 0:
            j = i // N_MIXERS
            x = x + _fourier_mixer(xn, p['fnet_w_out'][j], p['fnet_b_out'][j])
        else:
            j = i // N_MIXERS
            x = x + _rglru_mixer(xn, p['lru_w_in'][j], p['lru_conv_w'][j], p['lru_conv_b'][j],
                                 p['lru_w_a'][j], p['lru_b_a'][j], p['lru_w_i'][j], p['lru_b_i'][j],
                                 p['lru_lambda'][j], p['lru_w_out'][j])
        xn = _rmsnorm(x, p['norm_xa'][i])
        mn = _rmsnorm(mem, p['norm_mem'][i])
        x = x + _cross_attention(xn, mn, p['xa_w_q'][i], p['xa_w_kv'][i], p['xa_w_o'][i])
        xn = _rmsnorm(x, p['norm_ffn'][i])
        x = x + _conv_ffn(xn, p['ffn_w_up'][i], p['ffn_conv_w'][i], p['ffn_conv_b'][i], p['ffn_w_down'][i])
    return _rmsnorm(x, p['norm_final'])


def setup_inputs(seed: int = 0) -> dict:
    key = jax.random.key(seed)
    ks = jax.random.split(key, 32)
    f32 = jnp.float32

    def nrm(k, shape, fan_in):
        return jax.random.normal(k, shape, f32) * (fan_in ** -0.5)

    def gain(k, shape):
        return 1.0 + 0.02 * jax.random.normal(k, shape, f32)

    def small(k, shape):
        return 0.01 * jax.random.normal(k, shape, f32)

    u = jax.random.uniform(ks[12], (N_LRU_LAYERS, 2, D_RNN), f32, 0.9, 0.999)
    a0 = u ** (1.0 / LRU_C)
    lam = jnp.log(a0) - jnp.log1p(-a0)
    return {
        'x_prompt': jax.random.normal(ks[0], (BATCH, SEQ, D_MODEL), f32),
        'x_sample': jax.random.normal(ks[1], (DEC_BATCH, DEC_SEQ, D_MODEL), f32),
        'mem_prompt': jax.random.normal(ks[2], (BATCH, N_MEM, D_MODEL), f32),
        'mem_sample': jax.random.normal(ks[3], (DEC_BATCH, N_MEM, D_MODEL), f32),
        'norm_mix': gain(ks[4], (DEPTH, D_MODEL)),
        'fnet_w_out': nrm(ks[5], (N_FNET_LAYERS, D_MODEL, D_MODEL), D_MODEL),
        'fnet_b_out': small(ks[6], (N_FNET_LAYERS, D_MODEL)),
        'lru_w_in': nrm(ks[7], (N_LRU_LAYERS, D_MODEL, 2 * D_RNN), D_MODEL),
        'lru_conv_w': nrm(ks[8], (N_LRU_LAYERS, LRU_CONV, D_RNN), LRU_CONV),
        'lru_conv_b': small(ks[9], (N_LRU_LAYERS, D_RNN)),
        'lru_w_a': nrm(ks[10], (N_LRU_LAYERS, 2, LRU_BLOCKS, LRU_BLOCK, LRU_BLOCK), LRU_BLOCK),
        'lru_b_a': small(ks[11], (N_LRU_LAYERS, 2, D_RNN)),
        'lru_w_i': nrm(ks[13], (N_LRU_LAYERS, 2, LRU_BLOCKS, LRU_BLOCK, LRU_BLOCK), LRU_BLOCK),
        'lru_b_i': small(ks[14], (N_LRU_LAYERS, 2, D_RNN)),
        'lru_lambda': lam,
        'lru_w_out': nrm(ks[15], (N_LRU_LAYERS, D_RNN, D_MODEL), D_RNN),
        'norm_xa': gain(ks[16], (DEPTH, D_MODEL)),
        'norm_mem': gain(ks[17], (DEPTH, D_MODEL)),
        'xa_w_q': nrm(ks[18], (DEPTH, D_MODEL, D_MODEL), D_MODEL),
        'xa_w_kv': nrm(ks[19], (DEPTH, D_MODEL, 2 * D_MODEL), D_MODEL),
        'xa_w_o': nrm(ks[20], (DEPTH, D_MODEL, D_MODEL), D_MODEL),
        'norm_ffn': gain(ks[21], (DEPTH, D_MODEL)),
        'ffn_w_up': nrm(ks[22], (DEPTH, D_MODEL, 2 * D_FF), D_MODEL),
        'ffn_conv_w': nrm(ks[23], (DEPTH, FFN_CONV, D_FF), FFN_CONV),
        'ffn_conv_b': small(ks[24], (DEPTH, D_FF)),
        'ffn_w_down': nrm(ks[25], (DEPTH, D_FF, D_MODEL), D_FF),
        'norm_final': gain(ks[26], (D_MODEL,)),
    }


def reference(x_prompt, x_sample, mem_prompt, mem_sample, norm_mix, fnet_w_out, fnet_b_out,
              lru_w_in, lru_conv_w, lru_conv_b, lru_w_a, lru_b_a, lru_w_i, lru_b_i, lru_lambda,
              lru_w_out, norm_xa, norm_mem, xa_w_q, xa_w_kv, xa_w_o, norm_ffn, ffn_w_up,
              ffn_conv_w, ffn_conv_b, ffn_w_down, norm_final):
    params = {
        'norm_mix': norm_mix, 'fnet_w_out': fnet_w_out, 'fnet_b_out': fnet_b_out,
        'lru_w_in': lru_w_in, 'lru_conv_w': lru_conv_w, 'lru_conv_b': lru_conv_b,
        'lru_w_a': lru_w_a, 'lru_b_a': lru_b_a, 'lru_w_i': lru_w_i, 'lru_b_i': lru_b_i,
        'lru_lambda': lru_lambda, 'lru_w_out': lru_w_out,
        'norm_xa': norm_xa, 'norm_mem': norm_mem, 'xa_w_q': xa_w_q, 'xa_w_kv': xa_w_kv, 'xa_w_o': xa_w_o,
        'norm_ffn': norm_ffn, 'ffn_w_up': ffn_w_up, 'ffn_conv_w': ffn_conv_w, 'ffn_conv_b': ffn_conv_b,
        'ffn_w_down': ffn_w_down, 'norm_final': norm_final,
    }
    y_prompt = _trunk(x_prompt, mem_prompt, params)
    y_sample = _trunk(x_sample, mem_sample, params)
    return (y_prompt, y_sample)
```

```python
import numpy as np
import ml_dtypes
from contextlib import ExitStack
import concourse.bass as bass
import concourse.mybir as mybir
from concourse.bass_utils import run_bass_kernel_spmd

F32 = mybir.dt.float32
BF16 = mybir.dt.bfloat16
AF = mybir.ActivationFunctionType
ALU = mybir.AluOpType

S = 2048
D = 1024
KC = 8
NMEM = 256
DRNN = 1280
NJ = 10
DFF = 2816
NF = 22
NCORES = 8
EPS = 1e-6

ENGS = ("pe", "act", "dve", "pool", "sp")


class Buf:
    __slots__ = ("name", "w", "r")

    def __init__(self, name):
        self.name = name
        self.w = None
        self.r = []


class Op:
    __slots__ = ("eng", "fn", "deps", "pos", "is_dma", "needs_inc", "val", "sem_id", "clock", "waits", "dma_prev")

    def __init__(self, eng, fn, is_dma):
        self.eng = eng
        self.fn = fn
        self.is_dma = is_dma
        self.deps = []
        self.needs_inc = False
        self.val = None
        self.sem_id = None
        self.waits = []
        self.dma_prev = None
        self.clock = None


class Prog:
    def __init__(self, n_dma_sems=12):
        self.streams = {e: [] for e in ENGS}
        self.bufs = {}
        self.n_dma_sems = n_dma_sems
        self.dma_count = 0
        self.dma_last = [None] * n_dma_sems
        self.all_ops = []

    def B(self, *key):
        b = self.bufs.get(key)
        if b is None:
            b = Buf(key)
            self.bufs[key] = b
        return b

    def _add(self, op, reads, writes):
        deps = {}
        e = op.eng
        for b in reads:
            if b.w is not None:
                deps[b.w] = True
        for b in writes:
            w = b.w
            if w is not None and not (e == "pe" and w.eng == "pe"):
                deps[w] = True
            for r in b.r:
                if not (e == "pe" and r.eng == "pe"):
                    deps[r] = True
        deps.pop(op, None)
        op.deps = list(deps.keys())
        for b in reads:
            b.r.append(op)
        for b in writes:
            b.w = op
            b.r = []
        op.pos = len(self.streams[e])
        self.streams[e].append(op)
        self.all_ops.append(op)
        return op

    def op(self, eng, fn, reads=(), writes=()):
        return self._add(Op(eng, fn, False), reads, writes)

    def dma(self, fn, reads=(), writes=(), eng="sp"):
        o = Op(eng, fn, True)
        slot = self.dma_count % self.n_dma_sems
        o.sem_id = ("dma", slot)
        o.val = 16 * (self.dma_count // self.n_dma_sems + 1)
        o.dma_prev = self.dma_last[slot]
        self.dma_last[slot] = o
        self.dma_count += 1
        return self._add(o, reads, writes)

    def analyze(self):
        eng_clock = {e: {} for e in ENGS}
        for op in self.all_ops:
            ck = eng_clock[op.eng]
            deps = list(op.deps)
            if op.is_dma and op.dma_prev is not None:
                deps.append(op.dma_prev)
            deps.sort(key=lambda d: -d.pos)
            for d in deps:
                if d.is_dma:
                    key, v = d.sem_id, d.val
                else:
                    key, v = d.eng, d.pos + 1
                    if d.eng == "pe" and op.eng == "pe":
                        continue
                if ck.get(key, 0) >= v:
                    continue
                op.waits.append(d)
                d.needs_inc = True
                for k2, v2 in d.clock.items():
                    if ck.get(k2, 0) < v2:
                        ck[k2] = v2
            c = dict(ck)
            if op.is_dma:
                c[op.sem_id] = op.val
            else:
                c[op.eng] = op.pos + 1
            op.clock = c
        for e in ENGS:
            n = 0
            for op in self.streams[e]:
                if op.is_dma:
                    continue
                if op.needs_inc:
                    n += 1
                    op.val = n
        for op in self.all_ops:
            op.clock = None
        self.stats = {e: (len(self.streams[e]), sum(len(o.waits) for o in self.streams[e]),
                          sum(1 for o in self.streams[e] if o.needs_inc)) for e in ENGS}

    def emit(self, block, sems, dma_sems):
        handles = {"pe": "tensor", "act": "scalar", "dve": "vector", "pool": "gpsimd", "sp": "sync"}

        def run_stream(e):
            def body(engh):
                for op in self.streams[e]:
                    for d in op.waits:
                        if d.is_dma:
                            engh.wait_ge(dma_sems[d.sem_id[1]], d.val)
                        else:
                            engh.wait_ge(sems[d.eng], d.val)
                    if op.fn is None:
                        continue
                    ins = op.fn(engh)
                    if op.is_dma:
                        ins.then_inc(dma_sems[op.sem_id[1]], 16)
                    elif op.needs_inc:
                        ins.then_inc(sems[op.eng], 1)
            return body

        for e in ENGS:
            if self.streams[e]:
                getattr(block, handles[e])(run_stream(e))


class Stream:
    def __init__(self, bld, name, views, bufs):
        self.bld = bld
        self.name = name
        self.views = views
        self.bufs = bufs
        self.n = len(views)
        self.recording = True
        self.rec = []
        self.plan = None
        self.i = 0
        self.issued = 0
        self.cross = True
        self.prev_auto = None
        self.released = set()

    def start_play(self, reps, cross=True):
        self.cross = cross
        self.replen = len(self.rec)
        self.plan = self.rec * reps
        self.recording = False
        self.i = 0
        self.issued = 0

    def _issue(self):
        while self.issued < len(self.plan) and self.issued <= self.i + self.n - 2:
            k = self.issued
            if k - self.n >= 0 and (k - self.n) not in self.released:
                break
            if not self.cross and self.i > 0 and k >= ((self.i - 1) // self.replen + 1) * self.replen:
                break
            s_ap, ne, keys = self.plan[k]
            dst = self.views[k % self.n][:, 0:ne]
            self.bld.dma(dst, s_ap, reads=[self.bld.P.B(*kk) for kk in keys], writes=self.bufs[k % self.n])
            self.issued += 1

    def pop(self, src, nelem, keys, hold=False):
        if self.recording:
            self.rec.append((src, nelem, keys))
            k = len(self.rec) - 1
            return self.views[k % self.n], self.bufs[k % self.n], k
        idx = self.i
        self.i += 1
        if self.prev_auto is not None:
            self.released.add(self.prev_auto)
        self.prev_auto = None if hold else idx
        self._issue()
        assert self.issued > idx, (self.name, idx, self.issued)
        return self.views[idx % self.n], self.bufs[idx % self.n], idx

    def release(self, idx):
        if self.recording:
            return
        self.released.add(idx)
        self._issue()


XT_OFF = 0
XN_OFF = 65536
RING_OFF = 98304
NRING = 3
RING_SLOT = 8192
AR_OFF = RING_OFF + NRING * RING_SLOT
AR_KB = 78
MISC_OFF = AR_OFF + AR_KB * 1024
IDENT_OFF = MISC_OFF
ONES1_OFF = IDENT_OFF + 512
ONESN_OFF = ONES1_OFF + 256
COLS_OFF = ONESN_OFF + 256
PR_OFF = COLS_OFF + 256
NPR = 640
SB_BYTES = PR_OFF + NPR * 4

PRC = {}
_c = 0
for _nm, _n in [("nmix1", 8), ("nxa0", 8), ("nxa1", 8), ("nffn0", 8), ("nffn1", 8), ("nmem0", 8), ("nmem1", 8),
                ("nfin", 8), ("fb", 8),
                ("lcw0", 10), ("lcw1", 10), ("lcw2", 10), ("lcw3", 10), ("lcb", 10),
                ("lba0", 10), ("lba1", 10), ("lbi0", 10), ("lbi1", 10), ("lk0", 10), ("lk1", 10),
                ("f0cw0", 22), ("f0cw1", 22), ("f0cw2", 22), ("f0cb", 22),
                ("f1cw0", 22), ("f1cw1", 22), ("f1cw2", 22), ("f1cb", 22)]:
    PRC[_nm] = _c
    _c += _n
assert _c <= NPR

FFN_GROUPS = [(0, 11), (11, 22)]


class Builder:
    def __init__(self, nseq=6, phases=None):
        self.nseq = nseq
        self.phases = phases
        self.P = None
        self.bank_i = 0
        self.quad_i = 0
        self.col_i = 0
        self.load = {"act": 0.0, "dve": 0.0, "pool": 0.0}

    def B(self, *k):
        return self.P.B(*k)

    def XT(self, kc, tb):
        return self.P.B("xt", kc, tb)

    def XN(self, kc, tb):
        return self.P.B("xn", kc, tb)

    def XTr(self, kc):
        return [self.P.B("xt", kc, t) for t in range(4)]

    def XNr(self, kc):
        return [self.P.B("xn", kc, t) for t in range(4)]

    def PS(self, b):
        return self.P.B("ps", b)

    def PSq(self, q):
        return [self.P.B("ps", q + t) for t in range(4)]

    def AR(self, a, b):
        return [self.P.B("ar", i) for i in range(int(a), int(np.ceil(b)))]

    def v32(self, off, n):
        return self.SB[:, off // 2: off // 2 + 2 * n].bitcast(F32)

    def v16(self, off, n):
        return self.SB[:, off // 2: off // 2 + n]

    def a32(self, kb, n):
        return self.v32(AR_OFF + int(kb * 1024), n)

    def a16(self, kb, n):
        return self.v16(AR_OFF + int(kb * 1024), n)

    def bank(self):
        b = self.bank_i
        self.bank_i = (self.bank_i + 1) % 8
        return b

    def quad(self):
        q = self.quad_i * 4
        self.quad_i ^= 1
        return q

    def psq(self, q):
        return self.ps[:, q:q + 4, :].rearrange("p b n -> p (b n)")

    def col(self):
        i = self.col_i
        self.col_i = (self.col_i + 1) % 32
        return self.COLS[:, i:i + 1], [self.P.B("col", i)]

    def pr(self, name, j=0):
        c = PRC[name] + j
        return self.PR[:, c:c + 1]

    def pick(self, cands, cost):
        e = min(cands, key=lambda x: self.load[x])
        self.load[e] += cost
        return e

    def dma(self, out, in_, reads, writes, nc_ok=False):
        if nc_ok:
            self.P.dma(lambda e: e.dma_start(out=out, in_=in_, allow_slow_non_contiguous=True), reads, writes)
        else:
            self.P.dma(lambda e: e.dma_start(out=out, in_=in_), reads, writes)

    def mm(self, out, lhsT, rhs, start, stop, reads, writes):
        self.P.op("pe", lambda e: e.matmul(out, lhsT=lhsT, rhs=rhs, start=start, stop=stop), reads, writes)

    def tr(self, out, in_, reads, writes):
        ident = self.IDENT
        self.P.op("pe", lambda e: e.transpose(out, in_, ident), reads + [self.B("ident")], writes)

    def act(self, out, in_, func, reads, writes, bias=None, scale=None, accum_out=None, cost=1.0):
        kw = {}
        if bias is not None:
            kw["bias"] = bias
        if scale is not None:
            kw["scale"] = scale
        if accum_out is not None:
            kw["accum_out"] = accum_out
        self.load["act"] += cost
        self.P.op("act", lambda e: e.activation(out=out, in_=in_, func=func, **kw), reads, writes)

    def tt(self, eng, out, in0, in1, op, reads, writes, cost=1.0):
        self.load[eng] += cost
        self.P.op(eng, lambda e: e.tensor_tensor(out=out, in0=in0, in1=in1, op=op), reads, writes)

    def ts(self, eng, out, in0, s1, s2, op0, op1, reads, writes, cost=1.0):
        self.load[eng] += cost
        if s2 is None:
            self.P.op(eng, lambda e: e.tensor_scalar(out=out, in0=in0, scalar1=s1, scalar2=None, op0=op0), reads, writes)
        else:
            self.P.op(eng, lambda e: e.tensor_scalar(out=out, in0=in0, scalar1=s1, scalar2=s2, op0=op0, op1=op1), reads, writes)

    def stt(self, out, in0, scalar, in1, op0, op1, reads, writes, cost=1.0):
        self.load["dve"] += cost
        self.P.op("dve", lambda e: e.scalar_tensor_tensor(out=out, in0=in0, scalar=scalar, in1=in1, op0=op0, op1=op1), reads, writes)

    def cp(self, eng, out, in_, reads, writes, scale=None, cost=1.0):
        if eng == "act":
            self.act(out, in_, AF.Copy, reads, writes, scale=scale, cost=cost)
        else:
            self.load[eng] += cost
            if scale is None:
                self.P.op(eng, lambda e: e.tensor_copy(out=out, in_=in_), reads, writes)
            else:
                self.P.op(eng, lambda e: e.tensor_scalar(out=out, in0=in_, scalar1=float(scale), scalar2=None, op0=ALU.mult), reads, writes)

    def recip(self, out, in_, reads, writes, cost=1.0):
        self.load["dve"] += cost
        self.P.op("dve", lambda e: e.reciprocal(out=out, in_=in_), reads, writes)

    def memset(self, eng, ap, val, writes):
        self.P.op(eng, lambda e: e.memset(ap, val), [], writes)

    def build(self):
        nc = bass.Bass("TRN2", target_bir_lowering=False)
        self.nc = nc
        nseq = self.nseq
        dt = {}

        def din(name, shape, dtype=F32):
            dt[name] = nc.dram_tensor(name, list(shape), dtype, kind="ExternalInput").ap()
            return dt[name]

        din("x", [nseq, S, D])
        din("mem", [nseq, NMEM, D])
        din("norm_mix", [2, D]); din("fnet_w_out", [1, D, D]); din("fnet_b_out", [1, D])
        din("lru_w_in", [1, D, 2 * DRNN]); din("lru_conv_w", [1, 4, DRNN]); din("lru_conv_b", [1, DRNN])
        din("lru_w_a", [1, 2, NJ, 128, 128]); din("lru_b_a", [1, 2, DRNN])
        din("lru_w_i", [1, 2, NJ, 128, 128]); din("lru_b_i", [1, 2, DRNN])
        din("lru_lambda", [1, 2, DRNN]); din("lru_w_out", [1, DRNN, D])
        din("norm_xa", [2, D]); din("norm_mem", [2, D])
        din("xa_w_q", [2, D, D]); din("xa_w_kv", [2, D, 2 * D]); din("xa_w_o", [2, D, D])
        din("norm_ffn", [2, D]); din("ffn_w_up", [2, D, 2 * DFF]); din("ffn_conv_w", [2, 3, DFF])
        din("ffn_conv_b", [2, DFF]); din("ffn_w_down", [2, DFF, D]); din("norm_final", [D])
        din("dftu", [8, 128, 8192], BF16)
        din("cdft", [2, 256, 256])
        din("dftn", [128, 16], BF16)
        din("ident", [128, 128])
        self.dt = dt
        self.y = nc.dram_tensor("y", [nseq, S, D], F32, kind="ExternalOutput").ap()

        def scr(name, nslab, ne):
            return nc.dram_tensor("scr_" + name, [nslab, 128, ne], BF16, kind="Internal").ap()
        self.scr = {"mix": scr("mix", 4, 4096), "lin": scr("lin", NJ, 2560), "lout": scr("lout", 5, 2048)}
        for l in range(2):
            self.scr["q%d" % l] = scr("q%d" % l, 2, 4096)
            self.scr["o%d" % l] = scr("o%d" % l, 2, 4096)
            self.scr["kv%d" % l] = scr("kv%d" % l, 4, 4096)
            self.scr["up%d" % l] = scr("up%d" % l, NF, 2048)
            self.scr["dn%d" % l] = scr("dn%d" % l, 8, 2816)

        with ExitStack() as es:
            self.SB = es.enter_context(nc.sbuf_tensor("SB", [128, SB_BYTES // 2], BF16))
            self.ps = es.enter_context(nc.psum_tensor("ps", [128, 8, 512], F32))
            sems = {e: es.enter_context(nc.semaphore("s_" + e)) for e in ENGS}
            NDS = 12
            dsems = [es.enter_context(nc.semaphore("d%d" % i)) for i in range(NDS)]
            block = es.enter_context(nc.Block())

            self.XTv = self.v32(XT_OFF, KC * S).rearrange("p (k s) -> p k s", k=KC)
            self.XNF = self.v16(XN_OFF, KC * S).rearrange("p (k s) -> p k s", k=KC)
            self.XNT = self.v16(XN_OFF, 16 * D).rearrange("p (i c) -> p i c", i=16)
            self.IDENT = self.v32(IDENT_OFF, 128)
            self.ONES1 = self.v16(ONES1_OFF, 128)
            self.ONESN = self.v16(ONESN_OFF, 128)
            self.COLS = self.v32(COLS_OFF, 64)
            self.PR = self.v32(PR_OFF, NPR)
            ring_views = [self.v16(RING_OFF + i * RING_SLOT, 4096) for i in range(NRING)]
            dft_views = [self.a16(0, 8192), self.a16(16, 8192)]

            self.P = Prog(NDS)
            self.ws = Stream(self, "w", ring_views, [[self.P.B("ring", i)] for i in range(NRING)])
            self.dfs = Stream(self, "dft", dft_views, [self.AR(0, 16), self.AR(16, 32)])
            self.sequence(0)
            rec_w, rec_d = self.ws.rec, self.dfs.rec

            self.P = Prog(NDS)
            self.bank_i = 0; self.quad_i = 0; self.col_i = 0
            self.load = {"act": 0.0, "dve": 0.0, "pool": 0.0}
            self.ws = Stream(self, "w", ring_views, [[self.P.B("ring", i)] for i in range(NRING)])
            self.dfs = Stream(self, "dft", dft_views, [self.AR(0, 16), self.AR(16, 32)])
            self.ws.rec = rec_w; self.dfs.rec = rec_d
            self.ws.start_play(nseq); self.dfs.start_play(nseq, cross=False)
            self.x_pre = None; self.pre_unit = None
            self.prologue()
            for seq in range(nseq):
                self.sequence(seq)
            self.P.op("sp", None, reads=[self.B("y", s, i) for s in range(nseq) for i in range(16)])
            self.P.analyze()
            self.stats = self.P.stats
            self.P.emit(block, sems, dsems)
        return nc

    def want(self, ph):
        return self.phases is None or ph in self.phases

    def prologue(self):
        dt = self.dt
        B = self.B
        self.dma(self.IDENT, dt["ident"], [], [B("ident")])
        self.memset("pool", self.ONES1, 1.0, [B("ones1")])
        self.memset("pool", self.ONESN, 1.0 / 1024.0, [B("onesn")])

        def ld(name, vec, n):
            c = PRC[name]
            self.dma(self.PR[:, c:c + n], vec.rearrange("(n p) -> p n", p=128), [], [B("pr", name)], nc_ok=True)
        ld("nmix1", dt["norm_mix"][1], 8)
        for l in range(2):
            ld("nxa%d" % l, dt["norm_xa"][l], 8)
            ld("nffn%d" % l, dt["norm_ffn"][l], 8)
            ld("nmem%d" % l, dt["norm_mem"][l], 8)
            for k in range(3):
                ld("f%dcw%d" % (l, k), dt["ffn_conv_w"][l, k], NF)
            ld("f%dcb" % l, dt["ffn_conv_b"][l], NF)
        ld("nfin", dt["norm_final"], 8)
        ld("fb", dt["fnet_b_out"][0], 8)
        for k in range(4):
            ld("lcw%d" % k, dt["lru_conv_w"][0, k], NJ)
        ld("lcb", dt["lru_conv_b"][0], NJ)
        for e in range(2):
            ld("lba%d" % e, dt["lru_b_a"][0, e], NJ)
            ld("lbi%d" % e, dt["lru_b_i"][0, e], NJ)
            ld("lk%d" % e, dt["lru_lambda"][0, e], NJ)
        c0 = PRC["lk0"]
        kap = self.PR[:, c0:c0 + 20]
        kb = [B("pr", "lk0"), B("pr", "lk1")]
        self.act(kap, kap, AF.Exp, kb, kb, scale=-1.0, cost=0.01)
        self.act(kap, kap, AF.Ln, kb, kb, bias=1.0, scale=1.0, cost=0.01)
        self.ts("dve", kap, kap, -4.0, None, ALU.mult, None, kb, kb, cost=0.01)
        for nm_ in ("lba0", "lba1", "lbi0", "lbi1"):
            cc_ = PRC[nm_]
            hb = self.PR[:, cc_:cc_ + 10]
            self.ts("dve", hb, hb, 0.5, None, ALU.mult, None, [B("pr", nm_)], [B("pr", nm_)], cost=0.01)

        self.cv_slab = 0
        self.cv_piece = 0

        def convert(scr_ap, slab_idx, ne, pieces, scr_buf):
            k = self.cv_slab % 3
            self.cv_slab += 1
            stb = self.a16(48 + 8 * k, 4096)
            stb_bufs = self.AR(48 + 8 * k, 56 + 8 * k)
            for (src, n_el, shape, dst_fn) in pieces:
                kp = self.cv_piece % 3
                self.cv_piece += 1
                stf = self.a32(16 * kp, 4096)
                stf_bufs = self.AR(16 * kp, 16 * kp + 16)
                a, b = shape
                stf_v = stf[:, 0:n_el].rearrange("p (a b) -> p a b", a=a)
                self.dma(stf_v, src, [], stf_bufs)
                eng = self.pick(["act", "dve"], n_el / 2048.0)
                out_v, in_v = dst_fn(stb, stf[:, 0:n_el])
                self.cp(eng, out_v, in_v, stf_bufs, stb_bufs, cost=0.0)
            if isinstance(slab_idx, slice):
                self.dma(scr_ap[slab_idx].rearrange("j p e -> p j e"), stb[:, 0:ne].rearrange("p (j e) -> p j e", j=2), stb_bufs, scr_buf)
            else:
                self.dma(scr_ap[slab_idx], stb[:, 0:ne], stb_bufs, [scr_buf])

        def plain(off, n_el):
            return lambda stb, stf: (stb[:, off:off + n_el], stf)

        for l in range(2):
            for nm, key, ncol in [("q", "xa_w_q", 1024), ("o", "xa_w_o", 1024), ("kv", "xa_w_kv", 2048)]:
                W = dt[key][l].rearrange("(k p) n -> p k n", p=128)
                for s_i in range(ncol // 512):
                    convert(self.scr["%s%d" % (nm, l)], s_i, 4096,
                            [(W[:, :, s_i * 512:(s_i + 1) * 512], 4096, (8, 512), plain(0, 4096))],
                            B("scr", "%s%d" % (nm, l), s_i))
            Wu = dt["ffn_w_up"][l].rearrange("(k p) n -> p k n", p=128)
            for jp in range(NF // 2):
                pcs = []
                for gv in range(2):
                    c0_ = gv * DFF + jp * 256

                    def ufn(stb, stf, gv=gv):
                        return (stb.rearrange("p (j g k c) -> p g k j c", j=2, g=2, k=8)[:, gv],
                                stf.rearrange("p (k j c) -> p k j c", k=8, j=2))
                    pcs.append((Wu[:, :, c0_:c0_ + 256], 2048, (8, 256), ufn))
                convert(self.scr["up%d" % l], slice(2 * jp, 2 * jp + 2), 4096, pcs,
                        [B("scr", "up%d" % l, 2 * jp), B("scr", "up%d" % l, 2 * jp + 1)])
            Wd = dt["ffn_w_down"][l].rearrange("(k p) n -> p k n", p=128)
            for gi, (g0, g1) in enumerate(FFN_GROUPS):
                gl = g1 - g0
                for mp in range(4):
                    ne = 2 * gl * 128

                    def dfn(stb, stf, gl=gl, ne=ne):
                        return (stb[:, 0:ne].rearrange("p (m k c) -> p k m c", m=2, k=gl),
                                stf.rearrange("p (k m c) -> p k m c", k=gl, m=2))
                    convert(self.scr["dn%d" % l], gi * 4 + mp, ne,
                            [(Wd[:, g0:g1, mp * 256:(mp + 1) * 256], ne, (gl, 256), dfn)], B("scr", "dn%d" % l, gi * 4 + mp))
        Wi = dt["lru_w_in"][0].rearrange("(k p) n -> p k n", p=128)
        for j in range(NJ):
            pcs = [(Wi[:, :, DRNN + j * 128:DRNN + (j + 1) * 128], 1024, (8, 128), plain(0, 1024)),
                   (Wi[:, :, j * 128:(j + 1) * 128], 1024, (8, 128), plain(1024, 1024)),
                   (dt["lru_w_a"][0][:, j].rearrange("e i o -> i e o"), 256, (2, 128), plain(2048, 256)),
                   (dt["lru_w_i"][0][:, j].rearrange("e i o -> i e o"), 256, (2, 128), plain(2304, 256))]
            convert(self.scr["lin"], j, 2560, pcs, B("scr", "lin", j))
        Wo = dt["lru_w_out"][0].rearrange("(k p) n -> p k n", p=128)
        for g in range(5):
            convert(self.scr["lout"], g, 2048, [(Wo[:, 2 * g:2 * g + 2, :], 2048, (2, 1024), plain(0, 2048))], B("scr", "lout", g))

        WF = self.a32(0, 8192).rearrange("p (k n) -> p k n", k=8)
        wf_bufs = self.AR(0, 32)
        CC = self.a32(32, 1024).rearrange("p (t a c) -> p t a c", t=2, a=2)
        cc_bufs = self.AR(32, 36)
        self.dma(WF, dt["fnet_w_out"][0].rearrange("(k p) n -> p k n", p=128), [], wf_bufs)
        self.dma(CC, dt["cdft"].rearrange("t (a p) c -> p t a c", p=128), [], cc_bufs)
        MIXS = self.v16(XN_OFF, 16384).rearrange("p (s t c) -> p s t c", s=4, t=16)
        mix_bufs = [self.XN(k, t) for k in range(8) for t in range(4)]
        for trig in range(2):
            for kco in range(8):
                gq, b2 = kco // 2, kco % 2
                for nb in range(2):
                    bk = self.bank()
                    for a in range(2):
                        self.mm(self.ps[:, bk, :], CC[:, trig, a, b2 * 128:(b2 + 1) * 128], WF[:, gq * 2 + a, nb * 512:(nb + 1) * 512],
                                a == 0, a == 1, wf_bufs + cc_bufs, [self.PS(bk)])
                    eng = self.pick(["act", "dve"], 0.25)
                    self.cp(eng, MIXS[:, nb * 2:nb * 2 + 2, trig * 8 + kco, :], self.ps[:, bk, :].rearrange("p (s c) -> p s c", s=2),
                            [self.PS(bk)], mix_bufs, cost=0.0)
        for s_i in range(4):
            self.dma(self.scr["mix"][s_i], MIXS[:, s_i].rearrange("p t c -> p (t c)"), mix_bufs, [B("scr", "mix", s_i)])

    def sequence(self, seq):
        if self.want("fnet"):
            self.load_x(seq)
            self.fnet()
        elif self.want("load"):
            self.load_x(seq)
        if self.want("xa0"):
            self.xa_kv(0, seq)
            self.fnorm("nxa0")
            self.xa(0, seq)
        if self.want("ffn0"):
            self.fnorm("nffn0")
            self.ffn(0)
        if self.want("lru"):
            self.fnorm("nmix1")
            self.lru()
        if self.want("xa1"):
            self.xa_kv(1, seq)
            self.fnorm("nxa1")
            self.xa(1, seq)
        if self.want("ffn1"):
            self.fnorm("nffn1")
            self.ffn(1)
        self.final(seq)

    def load_x(self, seq):
        dt = self.dt
        GB = self.a32(64, 1024)
        gb_bufs = self.AR(64, 68)
        self.dma(GB, dt["norm_mix"][0:1, :].broadcast_to([128, D]), [], gb_bufs)
        for i in range(16):
            k = i % 4
            stg = self.a32(48 + 4 * k, 1024)
            stg_bufs = self.AR(48 + 4 * k, 52 + 4 * k)
            if not (i < 4 and getattr(self, "x_pre", None) == seq):
                self.dma(stg, dt["x"][seq, i * 128:(i + 1) * 128, :], [], stg_bufs)
            xnb = [self.XN(i // 2, 2 * (i % 2)), self.XN(i // 2, 2 * (i % 2) + 1)]
            css, cssb = self.col()
            crs, crsb = self.col()
            self.act(self.XNT[:, i, :], stg, AF.Square, stg_bufs, xnb + cssb, accum_out=css, cost=0.5)
            self.act(crs, css, AF.Sqrt, cssb, crsb, bias=EPS, scale=1.0 / D, cost=0.05)
            self.recip(crs, crs, crsb, crsb, cost=0.05)
            self.stt(self.XNT[:, i, :], stg, crs, GB, ALU.mult, ALU.mult, stg_bufs + crsb + gb_bufs, xnb, cost=0.5)
            for half in range(2):
                bk = self.bank()
                for cc in range(4):
                    c = half * 4 + cc
                    self.tr(self.ps[:, bk, cc * 128:(cc + 1) * 128], stg[:, c * 128:(c + 1) * 128], stg_bufs, [self.PS(bk)])
                eng = "dve"
                self.cp(eng, self.XTv[:, half * 4:half * 4 + 4, i * 128:(i + 1) * 128],
                        self.ps[:, bk, :].rearrange("p (c t) -> p c t", c=4), [self.PS(bk)],
                        [self.XT(half * 4 + cc, i // 4) for cc in range(4)], cost=0.0)

    def fnet(self):
        dt = self.dt
        Zb = lambda z, trig, c: self.AR(32 + 16 * z + trig * 8 + c, 33 + 16 * z + trig * 8 + c)
        Zv = lambda z, trig, c: self.a16(32 + 16 * z + trig * 8 + c, 512)
        Mb = lambda z, trig, c: self.AR(16 * z + trig * 8 + c, 16 * z + trig * 8 + c + 1)
        Mv = lambda z, trig, c: self.a16(16 * z + trig * 8 + c, 512)
        NYQ = self.a16(64, 8)
        nyqb = self.AR(64, 65)
        DFTN = self.a16(65, 16)
        dftnb = self.AR(65, 66)
        self.dma(DFTN, dt["dftn"], [], dftnb)

        def mix(sb, rv, rb):
            for m in range(8):
                if m % 2 == 0:
                    slab, sbufs, _ = self.ws.pop(self.scr["mix"][m // 2], 4096, [("scr", "mix", m // 2)])
                    slab = slab.rearrange("p (t c) -> p t c", t=16)
                bk = self.bank()
                for t in range(16):
                    trig, kc = t // 8, t % 8
                    self.mm(self.ps[:, bk, :], slab[:, t, (m % 2) * 128:(m % 2 + 1) * 128], rv(trig, kc), t == 0, t == 15,
                            sbufs + rb(trig, kc), [self.PS(bk)])
                xs = self.XTv[:, m, sb * 512:(sb + 1) * 512]
                self.stt(xs, self.ps[:, bk, :], self.pr("fb", m), xs, ALU.add, ALU.add,
                         [self.PS(bk), self.B("pr", "fb"), self.XT(m, sb)], [self.XT(m, sb)], cost=0.25)

        for z in range(2):
            for trig in range(2):
                u = z * 2 + trig
                if u == 0 and getattr(self, "pre_unit", None) is not None:
                    unit, ubufs, _ = self.pre_unit
                    self.pre_unit = None
                else:
                    unit, ubufs, _ = self.dfs.pop(dt["dftu"][u], 8192, [])
                unit = unit.rearrange("p (k j) -> p k j", k=16)
                for c in range(8):
                    bk = self.bank()
                    for kc in range(16):
                        xnb = [self.XN(kc // 2, 2 * (kc % 2)), self.XN(kc // 2, 2 * (kc % 2) + 1)]
                        self.mm(self.ps[:, bk, :], self.XNT[:, kc, c * 128:(c + 1) * 128], unit[:, kc, :], kc == 0, kc == 15,
                                xnb + ubufs, [self.PS(bk)])
                    eng = self.pick(["act", "dve"], 0.25)
                    self.cp(eng, Zv(z, trig, c), self.ps[:, bk, :], [self.PS(bk)], Zb(z, trig, c))
            if z == 1:
                bk = self.bank()
                for c in range(8):
                    for kc in range(16):
                        xnb = [self.XN(kc // 2, 2 * (kc % 2)), self.XN(kc // 2, 2 * (kc % 2) + 1)]
                        self.mm(self.ps[:, bk, c:c + 1], self.XNT[:, kc, c * 128:(c + 1) * 128], DFTN[:, kc:kc + 1], kc == 0, kc == 15,
                                xnb + dftnb, [self.PS(bk)])
                self.cp("dve", NYQ, self.ps[:, bk, 0:8], [self.PS(bk)], nyqb, cost=0.05)
            mix(z, lambda trig, kc, z=z: Zv(z, trig, kc), lambda trig, kc, z=z: Zb(z, trig, kc))
        for zz in range(2):
            for trig in range(2):
                sgn = 1.0 if trig == 0 else -1.0
                for c in range(8):
                    mv, mb = Mv(zz, trig, c), Mb(zz, trig, c)
                    eng = self.pick(["act", "dve"], 0.25)
                    if zz == 0:
                        if trig == 0:
                            self.cp(eng, mv[:, 0:1], NYQ[:, c:c + 1], nyqb, mb, cost=0.0)
                        else:
                            self.memset("pool", mv[:, 0:1], 0.0, mb)
                        src, srcb = Zv(1, trig, c), Zb(1, trig, c)
                    else:
                        self.cp(eng, mv[:, 0:1], Zv(1, trig, c)[:, 0:1], Zb(1, trig, c), mb, scale=(None if trig == 0 else -1.0), cost=0.0)
                        src, srcb = Zv(0, trig, c), Zb(0, trig, c)
                    self.cp(eng, mv[:, 1:512], src[:, 511:0:-1], srcb, mb, scale=(None if trig == 0 else -1.0), cost=0.0)
            mix(2 + zz, lambda trig, kc, zz=zz: Mv(zz, trig, kc), lambda trig, kc, zz=zz: Mb(zz, trig, kc))

    def fnorm(self, gname, inplace=False):
        q = self.quad()
        for kc in range(8):
            k2 = kc % 2
            sq = self.a16(56 + 4 * k2, 2048)
            sqb = self.AR(56 + 4 * k2, 60 + 4 * k2)
            eng = "act" if kc % 2 == 0 else "dve"
            if eng == "act":
                self.act(sq, self.XTv[:, kc, :], AF.Square, self.XTr(kc), sqb, cost=0.0)
            else:
                self.tt("dve", sq, self.XTv[:, kc, :], self.XTv[:, kc, :], ALU.mult, self.XTr(kc), sqb, cost=0.0)
            for tb in range(4):
                self.mm(self.ps[:, q + tb, :], self.ONESN, sq[:, tb * 512:(tb + 1) * 512], kc == 0, kc == 7,
                        sqb + [self.B("onesn")], [self.PS(q + tb)])
        RS = self.a32(48, 2048)
        rsb = self.AR(48, 56)
        self.act(RS, self.psq(q), AF.Ln, self.PSq(q), rsb, bias=EPS, scale=1.0)
        self.act(RS, RS, AF.Exp, rsb, rsb, scale=-0.5)
        for kc in range(8):
            if inplace:
                self.stt(self.XTv[:, kc, :], self.XTv[:, kc, :], self.pr(gname, kc), RS, ALU.mult, ALU.mult,
                         self.XTr(kc) + rsb + [self.B("pr", gname)], self.XTr(kc))
            else:
                self.stt(self.XNF[:, kc, :], self.XTv[:, kc, :], self.pr(gname, kc), RS, ALU.mult, ALU.mult,
                         self.XTr(kc) + rsb + [self.B("pr", gname)], self.XNr(kc))

    def xa_views(self):
        QT = self.a16(0, 16384).rearrange("p (k s) -> p k s", k=8)
        KT = self.a16(32, 2048).rearrange("p (k m) -> p k m", k=8)
        V = self.a16(36, 2048).rearrange("p (c n) -> p c n", c=2)
        MNT = self.a16(64, 2048).rearrange("p (k m) -> p k m", k=8)
        return QT, KT, self.AR(32, 36), V, self.AR(36, 40), MNT, self.AR(64, 68)

    def xa_kv(self, l, seq):
        dt = self.dt
        B = self.B
        QT, KT, ktb, V, vb, MNT, mntb = self.xa_views()
        for mt in range(2):
            mstg = self.a32(68 + 4 * mt, 1024)
            msb = self.AR(68 + 4 * mt, 72 + 4 * mt)
            self.dma(mstg, dt["mem"][seq, mt * 128:(mt + 1) * 128, :], [], msb)
            css, cssb = self.col()
            crs, crsb = self.col()
            junk = self.a16(76, 1024)
            self.act(junk, mstg, AF.Square, msb, self.AR(76, 78) + cssb, accum_out=css, cost=0.5)
            self.act(crs, css, AF.Sqrt, cssb, crsb, bias=EPS, scale=1.0 / D, cost=0.05)
            self.recip(crs, crs, crsb, crsb, cost=0.05)
            self.act(mstg, mstg, AF.Copy, msb + crsb, msb, scale=crs, cost=0.5)
            for half in range(2):
                bk = self.bank()
                for cc in range(4):
                    c = half * 4 + cc
                    self.tr(self.ps[:, bk, cc * 128:(cc + 1) * 128], mstg[:, c * 128:(c + 1) * 128], msb, [self.PS(bk)])
                for cc in range(4):
                    c = half * 4 + cc
                    self.ts("dve", MNT[:, c, mt * 128:(mt + 1) * 128], self.ps[:, bk, cc * 128:(cc + 1) * 128],
                            self.pr("nmem%d" % l, c), None, ALU.mult, None, [self.PS(bk), B("pr", "nmem%d" % l)], mntb, cost=0.1)
        for hc in range(8):
            if hc % 4 == 0:
                slab, sbufs, _ = self.ws.pop(self.scr["kv%d" % l][hc // 4], 4096, [("scr", "kv%d" % l, hc // 4)])
                slab = slab.rearrange("p (k n) -> p k n", k=8)
            bk = self.bank()
            for kc in range(8):
                self.mm(self.ps[:, bk, 0:256], slab[:, kc, (hc % 4) * 128:(hc % 4 + 1) * 128], MNT[:, kc, :], kc == 0, kc == 7,
                        sbufs + mntb, [self.PS(bk)])
            eng = self.pick(["act", "dve"], 0.15)
            self.cp(eng, KT[:, hc, :], self.ps[:, bk, 0:256], [self.PS(bk)], ktb, cost=0.0)
        for nb in range(2):
            slab, sbufs, _ = self.ws.pop(self.scr["kv%d" % l][2 + nb], 4096, [("scr", "kv%d" % l, 2 + nb)])
            slab = slab.rearrange("p (k n) -> p k n", k=8)
            for mc in range(2):
                bk = self.bank()
                for kc in range(8):
                    self.mm(self.ps[:, bk, :], MNT[:, kc, mc * 128:(mc + 1) * 128], slab[:, kc, :], kc == 0, kc == 7,
                            sbufs + mntb, [self.PS(bk)])
                eng = self.pick(["act", "dve"], 0.25)
                self.cp(eng, V[:, mc, nb * 512:(nb + 1) * 512], self.ps[:, bk, :], [self.PS(bk)], vb, cost=0.0)

    def xa(self, l, seq):
        B = self.B
        QT, KT, ktb, V, vb, MNT, mntb = self.xa_views()
        qtb = lambda m, tb: self.AR(m * 4 + tb, m * 4 + tb + 1)
        for m in range(8):
            if m % 4 == 0:
                slab, sbufs, _ = self.ws.pop(self.scr["q%d" % l][m // 4], 4096, [("scr", "q%d" % l, m // 4)])
                slab = slab.rearrange("p (k n) -> p k n", k=8)
            q = self.quad()
            for kc in range(8):
                for tb in range(4):
                    self.mm(self.ps[:, q + tb, :], slab[:, kc, (m % 4) * 128:(m % 4 + 1) * 128], self.XNF[:, kc, tb * 512:(tb + 1) * 512],
                            kc == 0, kc == 7, sbufs + [self.XN(kc, tb)], [self.PS(q + tb)])
            eng = self.pick(["act", "dve"], 1.0)
            self.cp(eng, QT[:, m, :], self.psq(q), self.PSq(q), self.AR(m * 4, m * 4 + 4), scale=0.0625, cost=0.0)
        it = 0
        for h in range(4):
            for tb in range(4):
                par = it % 2
                it += 1
                PT = self.a16(40 + 2 * par, 1024).rearrange("p (c s) -> p c s", c=2)
                ptb = self.AR(40 + 2 * par, 42 + 2 * par)
                RSv = self.a32(44 + 2 * par, 512)
                rsb = self.AR(44 + 2 * par, 46 + 2 * par)
                for mc in range(2):
                    bk = self.bank()
                    for dc in range(2):
                        self.mm(self.ps[:, bk, :], KT[:, h * 2 + dc, mc * 128:(mc + 1) * 128], QT[:, h * 2 + dc, tb * 512:(tb + 1) * 512],
                                dc == 0, dc == 1, ktb + qtb(h * 2 + dc, tb), [self.PS(bk)])
                    self.act(PT[:, mc, :], self.ps[:, bk, :], AF.Exp, [self.PS(bk)], ptb, cost=0.25)
                bs = self.bank()
                for mc in range(2):
                    self.mm(self.ps[:, bs, :], self.ONES1, PT[:, mc, :], mc == 0, mc == 1, ptb + [B("ones1")], [self.PS(bs)])
                self.recip(RSv, self.ps[:, bs, :], [self.PS(bs)], rsb, cost=0.25)
                for dc in range(2):
                    bk = self.bank()
                    for mc in range(2):
                        self.mm(self.ps[:, bk, :], V[:, mc, (h * 2 + dc) * 128:(h * 2 + dc + 1) * 128], PT[:, mc, :], mc == 0, mc == 1,
                                vb + ptb, [self.PS(bk)])
                    self.tt("dve", self.XNF[:, h * 2 + dc, tb * 512:(tb + 1) * 512], self.ps[:, bk, :], RSv, ALU.mult,
                            [self.PS(bk)] + rsb, [self.XN(h * 2 + dc, tb)], cost=0.25)
        for m in range(8):
            if m % 4 == 0:
                slab, sbufs, _ = self.ws.pop(self.scr["o%d" % l][m // 4], 4096, [("scr", "o%d" % l, m // 4)])
                slab = slab.rearrange("p (k n) -> p k n", k=8)
            q = self.quad()
            for kc in range(8):
                for tb in range(4):
                    self.mm(self.ps[:, q + tb, :], slab[:, kc, (m % 4) * 128:(m % 4 + 1) * 128], self.XNF[:, kc, tb * 512:(tb + 1) * 512],
                            kc == 0, kc == 7, sbufs + [self.XN(kc, tb)], [self.PS(q + tb)])
            self.tt("dve", self.XTv[:, m, :], self.psq(q), self.XTv[:, m, :], ALU.add, self.PSq(q) + self.XTr(m), self.XTr(m))

    def ffn(self, l):
        B = self.B
        GBUF = self.a32(44, 2050)
        gbb = self.AR(44, 53)
        C = self.a32(53, 2048)
        cb = self.AR(53, 61)
        self.memset("pool", GBUF[:, 0:1], 0.0, gbb)
        self.memset("pool", GBUF[:, 2049:2050], 0.0, gbb)
        pre = "f%d" % l
        for gi, (g0, g1) in enumerate(FFN_GROUPS):
            gl = g1 - g0
            for j in range(g0, g1):
                slab, sbufs, _ = self.ws.pop(self.scr["up%d" % l][j], 2048, [("scr", "up%d" % l, j)])
                slab = slab[:, 0:2048].rearrange("p (g k c) -> p g k c", g=2, k=8)
                qg = self.quad()
                for kc in range(8):
                    for tb in range(4):
                        self.mm(self.ps[:, qg + tb, :], slab[:, 0, kc, :], self.XNF[:, kc, tb * 512:(tb + 1) * 512], kc == 0, kc == 7,
                                sbufs + [self.XN(kc, tb)], [self.PS(qg + tb)])
                self.act(GBUF[:, 1:2049], self.psq(qg), AF.Copy, self.PSq(qg), gbb)
                qv = self.quad()
                for kc in range(8):
                    for tb in range(4):
                        self.mm(self.ps[:, qv + tb, :], slab[:, 1, kc, :], self.XNF[:, kc, tb * 512:(tb + 1) * 512], kc == 0, kc == 7,
                                sbufs + [self.XN(kc, tb)], [self.PS(qv + tb)])
                self.ts("dve", C, GBUF[:, 0:2048], self.pr(pre + "cw0", j), self.pr(pre + "cb", j), ALU.mult, ALU.add,
                        gbb + [B("pr", pre + "cw0"), B("pr", pre + "cb")], cb)
                self.stt(C, GBUF[:, 1:2049], self.pr(pre + "cw1", j), C, ALU.mult, ALU.add, gbb + cb + [B("pr", pre + "cw1")], cb)
                self.stt(C, GBUF[:, 2:2050], self.pr(pre + "cw2", j), C, ALU.mult, ALU.add, gbb + cb + [B("pr", pre + "cw2")], cb)
                self.act(C, C, AF.Gelu_apprx_tanh, cb, cb)
                jj = j - g0
                self.tt("dve", self.a16(jj * 4, 2048), self.psq(qv), C, ALU.mult, self.PSq(qv) + cb, self.AR(jj * 4, jj * 4 + 4))
            for m in range(8):
                if m % 2 == 0:
                    ne = 2 * gl * 128
                    slab, sbufs, _ = self.ws.pop(self.scr["dn%d" % l][gi * 4 + m // 2][:, 0:ne], ne, [("scr", "dn%d" % l, gi * 4 + m // 2)])
                    slab = slab[:, 0:ne].rearrange("p (m k c) -> p m k c", m=2, k=gl)
                q = self.quad()
                for kk in range(gl):
                    hv = self.a16(kk * 4, 2048)
                    for tb in range(4):
                        self.mm(self.ps[:, q + tb, :], slab[:, m % 2, kk, :], hv[:, tb * 512:(tb + 1) * 512], kk == 0, kk == gl - 1,
                                sbufs + self.AR(kk * 4 + tb, kk * 4 + tb + 1), [self.PS(q + tb)])
                self.tt("dve", self.XTv[:, m, :], self.psq(q), self.XTv[:, m, :], ALU.add, self.PSq(q) + self.XTr(m), self.XTr(m))

    def lru(self):
        B = self.B
        REC = self.a32(0, 2051)
        recb = self.AR(0, 9)
        C = self.a32(9, 2048); cb = self.AR(9, 17)
        C16 = self.a16(17, 2048); c16b = self.AR(17, 21)
        X = [self.a32(21, 2048), self.a32(29, 2048)]
        xb = [self.AR(21, 29), self.AR(29, 37)]
        A = [self.a32(37, 2048), self.a32(45, 2048)]
        ab = [self.AR(37, 45), self.AR(45, 53)]
        T = [self.a32(53, 2048), self.a32(61, 2048)]
        tb_ = [self.AR(53, 61), self.AR(61, 69)]
        HG = lambda jj: self.a16(69 + 4 * jj, 2048)
        hgb = lambda jj: self.AR(69 + 4 * jj, 73 + 4 * jj)
        self.memset("pool", REC[:, 0:2], 0.0, recb)
        self.memset("pool", REC[:, 2050:2051], 0.0, recb)
        prb = [B("pr", n_) for n_ in ("lcw0", "lcw1", "lcw2", "lcw3", "lcb", "lba0", "lba1", "lbi0", "lbi1", "lk0", "lk1")]
        st = {}

        QA, QB = 0, 4

        def front_a(j):
            slab, sbufs, idx = self.ws.pop(self.scr["lin"][j], 2560, [("scr", "lin", j)], hold=True)
            st[j] = (slab, sbufs, idx)
            wrec = slab[:, 0:1024].rearrange("p (k c) -> p k c", k=8)
            q = QB
            for kc in range(8):
                for tb in range(4):
                    self.mm(self.ps[:, q + tb, :], wrec[:, kc, :], self.XNF[:, kc, tb * 512:(tb + 1) * 512], kc == 0, kc == 7,
                            sbufs + [self.XN(kc, tb)], [self.PS(q + tb)])
            self.act(REC[:, 2:2050], self.psq(q), AF.Copy, self.PSq(q), recb)
            self.act(C, REC[:, 0:2048], AF.Identity, recb + prb, cb, scale=self.pr("lcw0", j), bias=self.pr("lcb", j))
            for k in range(1, 4):
                self.stt(C, REC[:, k:k + 2048], self.pr("lcw%d" % k, j), C, ALU.mult, ALU.add, recb + cb + prb, cb)

        def front_b(j):
            self.act(C16, C, AF.Copy, cb, c16b)

        def gates(j, t, q):
            slab, sbufs, idx = st[j]
            w = slab[:, 2048 + t * 128:2048 + (t + 1) * 128]
            for tb in range(4):
                self.mm(self.ps[:, q + tb, :], w, C16[:, tb * 512:(tb + 1) * 512], True, True, sbufs + c16b, [self.PS(q + tb)])
            return q

        def mid_a(j):
            slab, sbufs, idx = st[j]
            wgate = slab[:, 1024:2048].rearrange("p (k c) -> p k c", k=8)
            qq = [gates(j, 0, QA), gates(j, 1, QB)]
            for e in range(2):
                qa = qq[e]
                self.act(A[e], self.psq(qa), AF.Tanh, self.PSq(qa) + prb, ab[e], bias=self.pr("lba%d" % e, j), scale=0.5)
                self.act(A[e], A[e], AF.Exp, ab[e] + prb, ab[e], scale=self.pr("lk%d" % e, j), bias=self.pr("lk%d" % e, j))
                self.act(T[e], A[e], AF.Square, ab[e], tb_[e])
                if e == 0:
                    qg = QA
                    for kc in range(8):
                        for tb in range(4):
                            self.mm(self.ps[:, qg + tb, :], wgate[:, kc, :], self.XNF[:, kc, tb * 512:(tb + 1) * 512], kc == 0, kc == 7,
                                    sbufs + [self.XN(kc, tb)], [self.PS(qg + tb)])
            for e in range(2):
                qi = gates(j, 2 + e, QB)
                self.act(X[e], self.psq(qi), AF.Tanh, self.PSq(qi) + prb, xb[e], bias=self.pr("lbi%d" % e, j), scale=0.5)
            self.ws.release(idx)
            for e in range(2):
                self.stt(X[e], X[e], 1.0, C, ALU.add, ALU.mult, xb[e] + cb, xb[e])
            return qg

        def mid_b(j, qg):
            for e in range(2):
                self.act(T[e], T[e], AF.Sqrt, tb_[e], tb_[e], bias=0.25, scale=-0.25)
            for e in range(2):
                self.tt("dve", T[e], T[e], X[e], ALU.mult, tb_[e] + xb[e], tb_[e], cost=1.0)
            self.act(X[0], self.psq(qg), AF.Gelu_apprx_tanh, self.PSq(qg), xb[0])
            T0, A0, T1, A1 = T[0], A[0], T[1], A[1]
            self.P.op("dve", lambda en: en.tensor_tensor_scan(out=T0, data0=A0, data1=T0, initial=0.0, op0=ALU.mult, op1=ALU.add),
                      ab[0] + tb_[0], tb_[0])
            self.P.op("dve", lambda en: en.tensor_tensor_scan(out=T1[:, ::-1], data0=A1[:, ::-1], data1=T1[:, ::-1], initial=0.0,
                                                              op0=ALU.mult, op1=ALU.add), ab[1] + tb_[1], tb_[1])
            self.load["dve"] += 4.0

        def wout(g):
            slab, sbufs, idx = self.ws.pop(self.scr["lout"][g], 2048, [("scr", "lout", g)], hold=True)
            slab = slab[:, 0:2048].rearrange("p (k n) -> p k n", k=2)
            for m in range(8):
                q = QB if m % 2 == 0 else QA
                for kk in range(2):
                    hv = HG(kk)
                    for tb in range(4):
                        self.mm(self.ps[:, q + tb, :], slab[:, kk, m * 128:(m + 1) * 128], hv[:, tb * 512:(tb + 1) * 512], kk == 0, kk == 1,
                                sbufs + self.AR(69 + 4 * kk + tb, 70 + 4 * kk + tb), [self.PS(q + tb)])
                self.tt("dve", self.XTv[:, m, :], self.psq(q), self.XTv[:, m, :], ALU.add, self.PSq(q) + self.XTr(m), self.XTr(m))
            self.ws.release(idx)

        front_a(0)
        front_b(0)
        for j in range(NJ):
            qg = mid_a(j)
            mid_b(j, qg)
            if j + 1 < NJ:
                front_a(j + 1)
            self.tt("dve", T[0], T[0], T[1], ALU.add, tb_[0] + tb_[1], tb_[0], cost=1.0)
            if j + 1 < NJ:
                front_b(j + 1)
            self.tt("dve", HG(j % 2), T[0], X[0], ALU.mult, tb_[0] + xb[0], hgb(j % 2))
            if j % 2 == 1:
                wout(j // 2)
        self.quad_i = 0

    def final(self, seq):
        self.fnorm("nfin", inplace=True)
        if seq + 1 < self.nseq and not self.ws.recording:
            dt = self.dt
            for i in range(4):
                stg = self.a32(48 + 4 * i, 1024)
                self.dma(stg, dt["x"][seq + 1, i * 128:(i + 1) * 128, :], [], self.AR(48 + 4 * i, 52 + 4 * i))
            self.x_pre = seq + 1
            self.pre_unit = self.dfs.pop(dt["dftu"][0], 8192, [])
        for i in range(16):
            k = i % 2
            ostg = self.a32(68 + 4 * k, 1024)
            ob = self.AR(68 + 4 * k, 72 + 4 * k)
            for half in range(2):
                bk = self.bank()
                for cc in range(4):
                    c = half * 4 + cc
                    self.tr(self.ps[:, bk, cc * 128:(cc + 1) * 128], self.XTv[:, c, i * 128:(i + 1) * 128], [self.XT(c, i // 4)], [self.PS(bk)])
                eng = "act" if half == 0 else "dve"
                self.cp(eng, ostg[:, half * 512:(half + 1) * 512], self.ps[:, bk, :], [self.PS(bk)], ob, cost=0.0)
            self.dma(self.y[seq, i * 128:(i + 1) * 128, :], ostg, ob, [self.B("y", seq, i)])


_CONST = {}


def _consts():
    if _CONST:
        return _CONST
    n = np.arange(S, dtype=np.int64)
    ang = ((n[:, None] * n[None, :]) % S).astype(np.float64) * (2.0 * np.pi / S)
    sc = 1.0 / np.sqrt(S)
    tabs = [np.cos(ang) * sc, np.sin(ang) * sc]
    units = np.empty((8, 128, 8192), dtype=ml_dtypes.bfloat16)
    for sb in range(4):
        for trig in range(2):
            t = tabs[trig][:, sb * 512:(sb + 1) * 512]
            t = t.reshape(16, 128, 512).transpose(1, 0, 2).reshape(128, 8192)
            units[sb * 2 + trig] = t.astype(ml_dtypes.bfloat16)
    m = np.arange(256, dtype=np.int64)
    ang2 = ((m[:, None] * m[None, :]) % 256).astype(np.float64) * (2.0 * np.pi / 256)
    cd = np.stack([np.cos(ang2) / 16.0, -np.sin(ang2) / 16.0]).astype(np.float32)
    nq = (np.where(n % 2 == 0, 1.0, -1.0) * sc).reshape(16, 128).T
    _CONST["dftn"] = np.ascontiguousarray(nq).astype(ml_dtypes.bfloat16)
    _CONST["dftu"] = units
    _CONST["cdft"] = cd
    _CONST["ident"] = np.eye(128, dtype=np.float32)
    return _CONST


_W_NAMES = ["norm_mix", "fnet_w_out", "fnet_b_out", "lru_w_in", "lru_conv_w", "lru_conv_b", "lru_w_a", "lru_b_a",
            "lru_w_i", "lru_b_i", "lru_lambda", "lru_w_out", "norm_xa", "norm_mem", "xa_w_q", "xa_w_kv", "xa_w_o",
            "norm_ffn", "ffn_w_up", "ffn_conv_w", "ffn_conv_b", "ffn_w_down", "norm_final"]

_NC_CACHE = {}


def kernel(**inputs):
    c = _consts()
    xp = np.asarray(inputs["x_prompt"], dtype=np.float32)
    xs = np.asarray(inputs["x_sample"], dtype=np.float32)
    mp = np.asarray(inputs["mem_prompt"], dtype=np.float32)
    ms = np.asarray(inputs["mem_sample"], dtype=np.float32)
    if "nc" not in _NC_CACHE:
        b = Builder(nseq=6)
        _NC_CACHE["nc"] = b.build()
    nc = _NC_CACHE["nc"]
    in_maps = []
    for i in range(NCORES):
        d = {"x": np.ascontiguousarray(np.concatenate([xp[2 * i:2 * i + 2], xs[4 * i:4 * i + 4]], axis=0)),
             "mem": np.ascontiguousarray(np.concatenate([mp[2 * i:2 * i + 2], ms[4 * i:4 * i + 4]], axis=0)),
             "dftu": c["dftu"], "cdft": c["cdft"], "ident": c["ident"], "dftn": c["dftn"]}
        for nm in _W_NAMES:
            d[nm] = np.ascontiguousarray(np.asarray(inputs[nm], dtype=np.float32))
        in_maps.append(d)
    res = run_bass_kernel_spmd(nc, in_maps, core_ids=list(range(NCORES)))
    ys = [np.asarray(r["y"]) for r in res.results]
    y_prompt = np.concatenate([y[0:2] for y in ys], axis=0).astype(np.float32)
    y_sample = np.concatenate([y[2:6] for y in ys], axis=0).astype(np.float32)
    return (y_prompt, y_sample)
```

```python
import numpy as np
import ml_dtypes
from contextlib import ExitStack
import concourse.bass as bass
import concourse.mybir as mybir
from concourse.bass_utils import run_bass_kernel_spmd

F32 = mybir.dt.float32
BF16 = mybir.dt.bfloat16
AF = mybir.ActivationFunctionType
ALU = mybir.AluOpType

S = 2048
D = 1024
KC = 8
NMEM = 256
DRNN = 1280
NJ = 10
DFF = 2816
NF = 22
NCORES = 8
EPS = 1e-6

ENGS = ("pe", "act", "dve", "pool", "sp")


class Buf:
    __slots__ = ("name", "w", "r")

    def __init__(self, name):
        self.name = name
        self.w = None
        self.r = []


class Op:
    __slots__ = ("eng", "fn", "deps", "pos", "is_dma", "needs_inc", "val", "sem_id", "clock", "waits", "dma_prev")

    def __init__(self, eng, fn, is_dma):
        self.eng = eng
        self.fn = fn
        self.is_dma = is_dma
        self.deps = []
        self.needs_inc = False
        self.val = None
        self.sem_id = None
        self.waits = []
        self.dma_prev = None
        self.clock = None


class Prog:
    def __init__(self, n_dma_sems=12):
        self.streams = {e: [] for e in ENGS}
        self.bufs = {}
        self.n_dma_sems = n_dma_sems
        self.dma_count = 0
        self.dma_last = [None] * n_dma_sems
        self.all_ops = []

    def B(self, *key):
        b = self.bufs.get(key)
        if b is None:
            b = Buf(key)
            self.bufs[key] = b
        return b

    def _add(self, op, reads, writes):
        deps = {}
        e = op.eng
        for b in reads:
            if b.w is not None:
                deps[b.w] = True
        for b in writes:
            w = b.w
            if w is not None and not (e == "pe" and w.eng == "pe"):
                deps[w] = True
            for r in b.r:
                if not (e == "pe" and r.eng == "pe"):
                    deps[r] = True
        deps.pop(op, None)
        op.deps = list(deps.keys())
        for b in reads:
            b.r.append(op)
        for b in writes:
            b.w = op
            b.r = []
        op.pos = len(self.streams[e])
        self.streams[e].append(op)
        self.all_ops.append(op)
        return op

    def op(self, eng, fn, reads=(), writes=()):
        return self._add(Op(eng, fn, False), reads, writes)

    def dma(self, fn, reads=(), writes=(), eng="sp"):
        o = Op(eng, fn, True)
        slot = self.dma_count % self.n_dma_sems
        o.sem_id = ("dma", slot)
        o.val = 16 * (self.dma_count // self.n_dma_sems + 1)
        o.dma_prev = self.dma_last[slot]
        self.dma_last[slot] = o
        self.dma_count += 1
        return self._add(o, reads, writes)

    def analyze(self):
        eng_clock = {e: {} for e in ENGS}
        for op in self.all_ops:
            ck = eng_clock[op.eng]
            deps = list(op.deps)
            if op.is_dma and op.dma_prev is not None:
                deps.append(op.dma_prev)
            deps.sort(key=lambda d: -d.pos)
            for d in deps:
                if d.is_dma:
                    key, v = d.sem_id, d.val
                else:
                    key, v = d.eng, d.pos + 1
                    if d.eng == "pe" and op.eng == "pe":
                        continue
                if ck.get(key, 0) >= v:
                    continue
                op.waits.append(d)
                d.needs_inc = True
                for k2, v2 in d.clock.items():
                    if ck.get(k2, 0) < v2:
                        ck[k2] = v2
            c = dict(ck)
            if op.is_dma:
                c[op.sem_id] = op.val
            else:
                c[op.eng] = op.pos + 1
            op.clock = c
        for e in ENGS:
            n = 0
            for op in self.streams[e]:
                if op.is_dma:
                    continue
                if op.needs_inc:
                    n += 1
                    op.val = n
        for op in self.all_ops:
            op.clock = None
        self.stats = {e: (len(self.streams[e]), sum(len(o.waits) for o in self.streams[e]),
                          sum(1 for o in self.streams[e] if o.needs_inc)) for e in ENGS}

    def emit(self, block, sems, dma_sems):
        handles = {"pe": "tensor", "act": "scalar", "dve": "vector", "pool": "gpsimd", "sp": "sync"}

        def run_stream(e):
            def body(engh):
                for op in self.streams[e]:
                    for d in op.waits:
                        if d.is_dma:
                            engh.wait_ge(dma_sems[d.sem_id[1]], d.val)
                        else:
                            engh.wait_ge(sems[d.eng], d.val)
                    if op.fn is None:
                        continue
                    ins = op.fn(engh)
                    if op.is_dma:
                        ins.then_inc(dma_sems[op.sem_id[1]], 16)
                    elif op.needs_inc:
                        ins.then_inc(sems[op.eng], 1)
            return body

        for e in ENGS:
            if self.streams[e]:
                getattr(block, handles[e])(run_stream(e))


class Stream:
    def __init__(self, bld, name, views, bufs):
        self.bld = bld
        self.name = name
        self.views = views
        self.bufs = bufs
        self.n = len(views)
        self.recording = True
        self.rec = []
        self.plan = None
        self.i = 0
        self.issued = 0
        self.cross = True
        self.prev_auto = None
        self.released = set()

    def start_play(self, reps, cross=True):
        self.cross = cross
        self.replen = len(self.rec)
        self.plan = self.rec * reps
        self.recording = False
        self.i = 0
        self.issued = 0

    def _issue(self):
        while self.issued < len(self.plan) and self.issued <= self.i + self.n - 2:
            k = self.issued
            if k - self.n >= 0 and (k - self.n) not in self.released:
                break
            if not self.cross and self.i > 0 and k >= ((self.i - 1) // self.replen + 1) * self.replen:
                break
            s_ap, ne, keys = self.plan[k]
            dst = self.views[k % self.n][:, 0:ne]
            self.bld.dma(dst, s_ap, reads=[self.bld.P.B(*kk) for kk in keys], writes=self.bufs[k % self.n])
            self.issued += 1

    def pop(self, src, nelem, keys, hold=False):
        if self.recording:
            self.rec.append((src, nelem, keys))
            k = len(self.rec) - 1
            return self.views[k % self.n], self.bufs[k % self.n], k
        idx = self.i
        self.i += 1
        if self.prev_auto is not None:
            self.released.add(self.prev_auto)
        self.prev_auto = None if hold else idx
        self._issue()
        assert self.issued > idx, (self.name, idx, self.issued)
        return self.views[idx % self.n], self.bufs[idx % self.n], idx

    def release(self, idx):
        if self.recording:
            return
        self.released.add(idx)
        self._issue()


XT_OFF = 0
XN_OFF = 65536
RING_OFF = 98304
NRING = 3
RING_SLOT = 8192
AR_OFF = RING_OFF + NRING * RING_SLOT
AR_KB = 78
MISC_OFF = AR_OFF + AR_KB * 1024
IDENT_OFF = MISC_OFF
ONES1_OFF = IDENT_OFF + 512
ONESN_OFF = ONES1_OFF + 256
COLS_OFF = ONESN_OFF + 256
PR_OFF = COLS_OFF + 256
NPR = 640
SB_BYTES = PR_OFF + NPR * 4

PRC = {}
_c = 0
for _nm, _n in [("nmix1", 8), ("nxa0", 8), ("nxa1", 8), ("nffn0", 8), ("nffn1", 8), ("nmem0", 8), ("nmem1", 8),
                ("nfin", 8), ("fb", 8),
                ("lcw0", 10), ("lcw1", 10), ("lcw2", 10), ("lcw3", 10), ("lcb", 10),
                ("lba0", 10), ("lba1", 10), ("lbi0", 10), ("lbi1", 10), ("lk0", 10), ("lk1", 10),
                ("f0cw0", 22), ("f0cw1", 22), ("f0cw2", 22), ("f0cb", 22),
                ("f1cw0", 22), ("f1cw1", 22), ("f1cw2", 22), ("f1cb", 22)]:
    PRC[_nm] = _c
    _c += _n
assert _c <= NPR

FFN_GROUPS = [(0, 11), (11, 22)]


class Builder:
    def __init__(self, nseq=6, phases=None):
        self.nseq = nseq
        self.phases = phases
        self.P = None
        self.bank_i = 0
        self.quad_i = 0
        self.col_i = 0
        self.load = {"act": 0.0, "dve": 0.0, "pool": 0.0}

    def B(self, *k):
        return self.P.B(*k)

    def XT(self, kc, tb):
        return self.P.B("xt", kc, tb)

    def XN(self, kc, tb):
        return self.P.B("xn", kc, tb)

    def XTr(self, kc):
        return [self.P.B("xt", kc, t) for t in range(4)]

    def XNr(self, kc):
        return [self.P.B("xn", kc, t) for t in range(4)]

    def PS(self, b):
        return self.P.B("ps", b)

    def PSq(self, q):
        return [self.P.B("ps", q + t) for t in range(4)]

    def AR(self, a, b):
        return [self.P.B("ar", i) for i in range(int(a), int(np.ceil(b)))]

    def v32(self, off, n):
        return self.SB[:, off // 2: off // 2 + 2 * n].bitcast(F32)

    def v16(self, off, n):
        return self.SB[:, off // 2: off // 2 + n]

    def a32(self, kb, n):
        return self.v32(AR_OFF + int(kb * 1024), n)

    def a16(self, kb, n):
        return self.v16(AR_OFF + int(kb * 1024), n)

    def bank(self):
        b = self.bank_i
        self.bank_i = (self.bank_i + 1) % 8
        return b

    def quad(self):
        q = self.quad_i * 4
        self.quad_i ^= 1
        return q

    def psq(self, q):
        return self.ps[:, q:q + 4, :].rearrange("p b n -> p (b n)")

    def col(self):
        i = self.col_i
        self.col_i = (self.col_i + 1) % 32
        return self.COLS[:, i:i + 1], [self.P.B("col", i)]

    def pr(self, name, j=0):
        c = PRC[name] + j
        return self.PR[:, c:c + 1]

    def pick(self, cands, cost):
        e = min(cands, key=lambda x: self.load[x])
        self.load[e] += cost
        return e

    def dma(self, out, in_, reads, writes, nc_ok=False, eng="sp"):
        if nc_ok:
            self.P.dma(lambda e: e.dma_start(out=out, in_=in_, allow_slow_non_contiguous=True), reads, writes, eng=eng)
        else:
            self.P.dma(lambda e: e.dma_start(out=out, in_=in_), reads, writes, eng=eng)

    def mm(self, out, lhsT, rhs, start, stop, reads, writes):
        self.P.op("pe", lambda e: e.matmul(out, lhsT=lhsT, rhs=rhs, start=start, stop=stop), reads, writes)

    def tr(self, out, in_, reads, writes):
        ident = self.IDENT
        self.P.op("pe", lambda e: e.transpose(out, in_, ident), reads + [self.B("ident")], writes)

    def act(self, out, in_, func, reads, writes, bias=None, scale=None, accum_out=None, cost=1.0):
        kw = {}
        if bias is not None:
            kw["bias"] = bias
        if scale is not None:
            kw["scale"] = scale
        if accum_out is not None:
            kw["accum_out"] = accum_out
        self.load["act"] += cost
        self.P.op("act", lambda e: e.activation(out=out, in_=in_, func=func, **kw), reads, writes)

    def tt(self, eng, out, in0, in1, op, reads, writes, cost=1.0):
        self.load[eng] += cost
        self.P.op(eng, lambda e: e.tensor_tensor(out=out, in0=in0, in1=in1, op=op), reads, writes)

    def ts(self, eng, out, in0, s1, s2, op0, op1, reads, writes, cost=1.0):
        self.load[eng] += cost
        if s2 is None:
            self.P.op(eng, lambda e: e.tensor_scalar(out=out, in0=in0, scalar1=s1, scalar2=None, op0=op0), reads, writes)
        else:
            self.P.op(eng, lambda e: e.tensor_scalar(out=out, in0=in0, scalar1=s1, scalar2=s2, op0=op0, op1=op1), reads, writes)

    def stt(self, out, in0, scalar, in1, op0, op1, reads, writes, cost=1.0):
        self.load["dve"] += cost
        self.P.op("dve", lambda e: e.scalar_tensor_tensor(out=out, in0=in0, scalar=scalar, in1=in1, op0=op0, op1=op1), reads, writes)

    def cp(self, eng, out, in_, reads, writes, scale=None, cost=1.0):
        if eng == "act":
            self.act(out, in_, AF.Copy, reads, writes, scale=scale, cost=cost)
        else:
            self.load[eng] += cost
            if scale is None:
                self.P.op(eng, lambda e: e.tensor_copy(out=out, in_=in_), reads, writes)
            else:
                self.P.op(eng, lambda e: e.tensor_scalar(out=out, in0=in_, scalar1=float(scale), scalar2=None, op0=ALU.mult), reads, writes)

    def recip(self, out, in_, reads, writes, cost=1.0):
        self.load["dve"] += cost
        self.P.op("dve", lambda e: e.reciprocal(out=out, in_=in_), reads, writes)

    def memset(self, eng, ap, val, writes):
        self.P.op(eng, lambda e: e.memset(ap, val), [], writes)

    def build(self):
        nc = bass.Bass("TRN2", target_bir_lowering=False)
        self.nc = nc
        nseq = self.nseq
        dt = {}

        def din(name, shape, dtype=F32):
            dt[name] = nc.dram_tensor(name, list(shape), dtype, kind="ExternalInput").ap()
            return dt[name]

        din("x", [nseq, S, D])
        din("mem", [nseq, NMEM, D])
        din("norm_mix", [2, D]); din("fnet_w_out", [1, D, D]); din("fnet_b_out", [1, D])
        din("lru_w_in", [1, D, 2 * DRNN]); din("lru_conv_w", [1, 4, DRNN]); din("lru_conv_b", [1, DRNN])
        din("lru_w_a", [1, 2, NJ, 128, 128]); din("lru_b_a", [1, 2, DRNN])
        din("lru_w_i", [1, 2, NJ, 128, 128]); din("lru_b_i", [1, 2, DRNN])
        din("lru_lambda", [1, 2, DRNN]); din("lru_w_out", [1, DRNN, D])
        din("norm_xa", [2, D]); din("norm_mem", [2, D])
        din("xa_w_q", [2, D, D]); din("xa_w_kv", [2, D, 2 * D]); din("xa_w_o", [2, D, D])
        din("norm_ffn", [2, D]); din("ffn_w_up", [2, D, 2 * DFF]); din("ffn_conv_w", [2, 3, DFF])
        din("ffn_conv_b", [2, DFF]); din("ffn_w_down", [2, DFF, D]); din("norm_final", [D])
        din("dftu", [8, 128, 8192], BF16)
        din("cdft", [2, 256, 256])
        din("dftn", [128, 16], BF16)
        din("ident", [128, 128])
        self.dt = dt
        self.y = nc.dram_tensor("y", [nseq, S, D], F32, kind="ExternalOutput").ap()

        def scr(name, nslab, ne):
            return nc.dram_tensor("scr_" + name, [nslab, 128, ne], BF16, kind="Internal").ap()
        self.scr = {"mix": scr("mix", 4, 4096), "lin": scr("lin", NJ, 2560), "lout": scr("lout", 5, 2048)}
        for l in range(2):
            self.scr["q%d" % l] = scr("q%d" % l, 2, 4096)
            self.scr["o%d" % l] = scr("o%d" % l, 2, 4096)
            self.scr["kv%d" % l] = scr("kv%d" % l, 4, 4096)
            self.scr["up%d" % l] = scr("up%d" % l, NF, 2048)
            self.scr["dn%d" % l] = scr("dn%d" % l, 8, 2816)

        with ExitStack() as es:
            self.SB = es.enter_context(nc.sbuf_tensor("SB", [128, SB_BYTES // 2], BF16))
            self.ps = es.enter_context(nc.psum_tensor("ps", [128, 8, 512], F32))
            sems = {e: es.enter_context(nc.semaphore("s_" + e)) for e in ENGS}
            NDS = 12
            dsems = [es.enter_context(nc.semaphore("d%d" % i)) for i in range(NDS)]
            block = es.enter_context(nc.Block())

            self.XTv = self.v32(XT_OFF, KC * S).rearrange("p (k s) -> p k s", k=KC)
            self.XNF = self.v16(XN_OFF, KC * S).rearrange("p (k s) -> p k s", k=KC)
            self.XNT = self.v16(XN_OFF, 16 * D).rearrange("p (i c) -> p i c", i=16)
            self.IDENT = self.v32(IDENT_OFF, 128)
            self.ONES1 = self.v16(ONES1_OFF, 128)
            self.ONESN = self.v16(ONESN_OFF, 128)
            self.COLS = self.v32(COLS_OFF, 64)
            self.PR = self.v32(PR_OFF, NPR)
            ring_views = [self.v16(RING_OFF + i * RING_SLOT, 4096) for i in range(NRING)]
            dft_views = [self.a16(0, 8192), self.a16(16, 8192)]

            self.P = Prog(NDS)
            self.ws = Stream(self, "w", ring_views, [[self.P.B("ring", i)] for i in range(NRING)])
            self.dfs = Stream(self, "dft", dft_views, [self.AR(0, 16), self.AR(16, 32)])
            self.sequence(0)
            rec_w, rec_d = self.ws.rec, self.dfs.rec

            self.P = Prog(NDS)
            self.bank_i = 0; self.quad_i = 0; self.col_i = 0
            self.load = {"act": 0.0, "dve": 0.0, "pool": 0.0}
            self.ws = Stream(self, "w", ring_views, [[self.P.B("ring", i)] for i in range(NRING)])
            self.dfs = Stream(self, "dft", dft_views, [self.AR(0, 16), self.AR(16, 32)])
            self.ws.rec = rec_w; self.dfs.rec = rec_d
            self.ws.start_play(nseq); self.dfs.start_play(nseq, cross=False)
            self.x_pre = None; self.pre_unit = None
            self.prologue()
            for seq in range(nseq):
                self.sequence(seq)
            self.P.op("sp", None, reads=[self.B("y", s, i) for s in range(nseq) for i in range(16)])
            self.P.analyze()
            self.stats = self.P.stats
            self.P.emit(block, sems, dsems)
        return nc

    def want(self, ph):
        return self.phases is None or ph in self.phases

    def prologue(self):
        dt = self.dt
        B = self.B
        self.dma(self.IDENT, dt["ident"], [], [B("ident")])
        self.memset("pool", self.ONES1, 1.0, [B("ones1")])
        self.memset("pool", self.ONESN, 1.0 / 1024.0, [B("onesn")])

        def ld(name, vec, n):
            c = PRC[name]
            self.dma(self.PR[:, c:c + n], vec.rearrange("(n p) -> p n", p=128), [], [B("pr", name)], nc_ok=True)
        ld("nmix1", dt["norm_mix"][1], 8)
        for l in range(2):
            ld("nxa%d" % l, dt["norm_xa"][l], 8)
            ld("nffn%d" % l, dt["norm_ffn"][l], 8)
            ld("nmem%d" % l, dt["norm_mem"][l], 8)
            for k in range(3):
                ld("f%dcw%d" % (l, k), dt["ffn_conv_w"][l, k], NF)
            ld("f%dcb" % l, dt["ffn_conv_b"][l], NF)
        ld("nfin", dt["norm_final"], 8)
        ld("fb", dt["fnet_b_out"][0], 8)
        for k in range(4):
            ld("lcw%d" % k, dt["lru_conv_w"][0, k], NJ)
        ld("lcb", dt["lru_conv_b"][0], NJ)
        for e in range(2):
            ld("lba%d" % e, dt["lru_b_a"][0, e], NJ)
            ld("lbi%d" % e, dt["lru_b_i"][0, e], NJ)
            ld("lk%d" % e, dt["lru_lambda"][0, e], NJ)
        c0 = PRC["lk0"]
        kap = self.PR[:, c0:c0 + 20]
        kb = [B("pr", "lk0"), B("pr", "lk1")]
        self.act(kap, kap, AF.Exp, kb, kb, scale=-1.0, cost=0.01)
        self.act(kap, kap, AF.Ln, kb, kb, bias=1.0, scale=1.0, cost=0.01)
        self.ts("dve", kap, kap, -4.0, None, ALU.mult, None, kb, kb, cost=0.01)
        for nm_ in ("lba0", "lba1", "lbi0", "lbi1"):
            cc_ = PRC[nm_]
            hb = self.PR[:, cc_:cc_ + 10]
            self.ts("dve", hb, hb, 0.5, None, ALU.mult, None, [B("pr", nm_)], [B("pr", nm_)], cost=0.01)

        self.cv_slab = 0
        self.cv_piece = 0

        def convert(scr_ap, slab_idx, ne, pieces, scr_buf):
            k = self.cv_slab % 3
            self.cv_slab += 1
            stb = self.a16(48 + 8 * k, 4096)
            stb_bufs = self.AR(48 + 8 * k, 56 + 8 * k)
            for (src, n_el, shape, dst_fn) in pieces:
                kp = self.cv_piece % 3
                self.cv_piece += 1
                stf = self.a32(16 * kp, 4096)
                stf_bufs = self.AR(16 * kp, 16 * kp + 16)
                a, b = shape
                stf_v = stf[:, 0:n_el].rearrange("p (a b) -> p a b", a=a)
                self.dma(stf_v, src, [], stf_bufs)
                eng = self.pick(["act", "dve"], n_el / 2048.0)
                out_v, in_v = dst_fn(stb, stf[:, 0:n_el])
                self.cp(eng, out_v, in_v, stf_bufs, stb_bufs, cost=0.0)
            if isinstance(slab_idx, slice):
                self.dma(scr_ap[slab_idx].rearrange("j p e -> p j e"), stb[:, 0:ne].rearrange("p (j e) -> p j e", j=2), stb_bufs, scr_buf,
                         eng="pool")
            else:
                self.dma(scr_ap[slab_idx], stb[:, 0:ne], stb_bufs, [scr_buf], eng="pool")

        def plain(off, n_el):
            return lambda stb, stf: (stb[:, off:off + n_el], stf)

        for l in range(2):
            for nm, key, ncol in [("q", "xa_w_q", 1024), ("o", "xa_w_o", 1024), ("kv", "xa_w_kv", 2048)]:
                W = dt[key][l].rearrange("(k p) n -> p k n", p=128)
                for s_i in range(ncol // 512):
                    convert(self.scr["%s%d" % (nm, l)], s_i, 4096,
                            [(W[:, :, s_i * 512:(s_i + 1) * 512], 4096, (8, 512), plain(0, 4096))],
                            B("scr", "%s%d" % (nm, l), s_i))
            Wu = dt["ffn_w_up"][l].rearrange("(k p) n -> p k n", p=128)
            for jp in range(NF // 2):
                pcs = []
                for gv in range(2):
                    c0_ = gv * DFF + jp * 256

                    def ufn(stb, stf, gv=gv):
                        return (stb.rearrange("p (j g k c) -> p g k j c", j=2, g=2, k=8)[:, gv],
                                stf.rearrange("p (k j c) -> p k j c", k=8, j=2))
                    pcs.append((Wu[:, :, c0_:c0_ + 256], 2048, (8, 256), ufn))
                convert(self.scr["up%d" % l], slice(2 * jp, 2 * jp + 2), 4096, pcs,
                        [B("scr", "up%d" % l, 2 * jp), B("scr", "up%d" % l, 2 * jp + 1)])
            Wd = dt["ffn_w_down"][l].rearrange("(k p) n -> p k n", p=128)
            for gi, (g0, g1) in enumerate(FFN_GROUPS):
                gl = g1 - g0
                for mp in range(4):
                    ne = 2 * gl * 128

                    def dfn(stb, stf, gl=gl, ne=ne):
                        return (stb[:, 0:ne].rearrange("p (m k c) -> p k m c", m=2, k=gl),
                                stf.rearrange("p (k m c) -> p k m c", k=gl, m=2))
                    convert(self.scr["dn%d" % l], gi * 4 + mp, ne,
                            [(Wd[:, g0:g1, mp * 256:(mp + 1) * 256], ne, (gl, 256), dfn)], B("scr", "dn%d" % l, gi * 4 + mp))
        Wi = dt["lru_w_in"][0].rearrange("(k p) n -> p k n", p=128)
        for j in range(NJ):
            pcs = [(Wi[:, :, DRNN + j * 128:DRNN + (j + 1) * 128], 1024, (8, 128), plain(0, 1024)),
                   (Wi[:, :, j * 128:(j + 1) * 128], 1024, (8, 128), plain(1024, 1024)),
                   (dt["lru_w_a"][0][:, j].rearrange("e i o -> i e o"), 256, (2, 128), plain(2048, 256)),
                   (dt["lru_w_i"][0][:, j].rearrange("e i o -> i e o"), 256, (2, 128), plain(2304, 256))]
            convert(self.scr["lin"], j, 2560, pcs, B("scr", "lin", j))
        Wo = dt["lru_w_out"][0].rearrange("(k p) n -> p k n", p=128)
        for g in range(5):
            convert(self.scr["lout"], g, 2048, [(Wo[:, 2 * g:2 * g + 2, :], 2048, (2, 1024), plain(0, 2048))], B("scr", "lout", g))

        WF = self.a32(0, 8192).rearrange("p (k n) -> p k n", k=8)
        wf_bufs = self.AR(0, 32)
        CC = self.a32(32, 1024).rearrange("p (t a c) -> p t a c", t=2, a=2)
        cc_bufs = self.AR(32, 36)
        self.dma(WF, dt["fnet_w_out"][0].rearrange("(k p) n -> p k n", p=128), [], wf_bufs)
        self.dma(CC, dt["cdft"].rearrange("t (a p) c -> p t a c", p=128), [], cc_bufs)
        MIXS = self.v16(XN_OFF, 16384).rearrange("p (s t c) -> p s t c", s=4, t=16)
        mix_bufs = [self.XN(k, t) for k in range(8) for t in range(4)]
        for trig in range(2):
            for kco in range(8):
                gq, b2 = kco // 2, kco % 2
                for nb in range(2):
                    bk = self.bank()
                    for a in range(2):
                        self.mm(self.ps[:, bk, :], CC[:, trig, a, b2 * 128:(b2 + 1) * 128], WF[:, gq * 2 + a, nb * 512:(nb + 1) * 512],
                                a == 0, a == 1, wf_bufs + cc_bufs, [self.PS(bk)])
                    eng = self.pick(["act", "dve"], 0.25)
                    self.cp(eng, MIXS[:, nb * 2:nb * 2 + 2, trig * 8 + kco, :], self.ps[:, bk, :].rearrange("p (s c) -> p s c", s=2),
                            [self.PS(bk)], mix_bufs, cost=0.0)
        for s_i in range(4):
            self.dma(self.scr["mix"][s_i], MIXS[:, s_i].rearrange("p t c -> p (t c)"), mix_bufs, [B("scr", "mix", s_i)])

    def sequence(self, seq):
        if self.want("fnet"):
            self.load_x(seq)
            self.fnet()
        elif self.want("load"):
            self.load_x(seq)
        if self.want("xa0"):
            self.xa_kv(0, seq)
            self.fnorm("nxa0")
            self.xa(0, seq)
        if self.want("ffn0"):
            self.fnorm("nffn0")
            self.ffn(0)
        if self.want("lru"):
            self.fnorm("nmix1")
            self.lru()
        if self.want("xa1"):
            self.xa_kv(1, seq)
            self.fnorm("nxa1")
            self.xa(1, seq)
        if self.want("ffn1"):
            self.fnorm("nffn1")
            self.ffn(1)
        self.final(seq)

    def load_x(self, seq):
        dt = self.dt
        GB = self.a32(64, 1024)
        gb_bufs = self.AR(64, 68)
        self.dma(GB, dt["norm_mix"][0:1, :].broadcast_to([128, D]), [], gb_bufs)
        for i in range(16):
            k = i % 4
            stg = self.a32(48 + 4 * k, 1024)
            stg_bufs = self.AR(48 + 4 * k, 52 + 4 * k)
            if not (i < 4 and getattr(self, "x_pre", None) == seq):
                self.dma(stg, dt["x"][seq, i * 128:(i + 1) * 128, :], [], stg_bufs)
            xnb = [self.XN(i // 2, 2 * (i % 2)), self.XN(i // 2, 2 * (i % 2) + 1)]
            css, cssb = self.col()
            crs, crsb = self.col()
            self.act(self.XNT[:, i, :], stg, AF.Square, stg_bufs, xnb + cssb, accum_out=css, cost=0.5)
            self.act(crs, css, AF.Sqrt, cssb, crsb, bias=EPS, scale=1.0 / D, cost=0.05)
            self.recip(crs, crs, crsb, crsb, cost=0.05)
            self.stt(self.XNT[:, i, :], stg, crs, GB, ALU.mult, ALU.mult, stg_bufs + crsb + gb_bufs, xnb, cost=0.5)
            for half in range(2):
                bk = self.bank()
                for cc in range(4):
                    c = half * 4 + cc
                    self.tr(self.ps[:, bk, cc * 128:(cc + 1) * 128], stg[:, c * 128:(c + 1) * 128], stg_bufs, [self.PS(bk)])
                eng = "dve"
                self.cp(eng, self.XTv[:, half * 4:half * 4 + 4, i * 128:(i + 1) * 128],
                        self.ps[:, bk, :].rearrange("p (c t) -> p c t", c=4), [self.PS(bk)],
                        [self.XT(half * 4 + cc, i // 4) for cc in range(4)], cost=0.0)

    def fnet(self):
        dt = self.dt
        Zb = lambda z, trig, c: self.AR(32 + 16 * z + trig * 8 + c, 33 + 16 * z + trig * 8 + c)
        Zv = lambda z, trig, c: self.a16(32 + 16 * z + trig * 8 + c, 512)
        Mb = lambda z, trig, c: self.AR(16 * z + trig * 8 + c, 16 * z + trig * 8 + c + 1)
        Mv = lambda z, trig, c: self.a16(16 * z + trig * 8 + c, 512)
        NYQ = self.a16(64, 8)
        nyqb = self.AR(64, 65)
        DFTN = self.a16(65, 16)
        dftnb = self.AR(65, 66)
        self.dma(DFTN, dt["dftn"], [], dftnb)

        def mix(sb, rv, rb):
            for m in range(8):
                if m % 2 == 0:
                    slab, sbufs, _ = self.ws.pop(self.scr["mix"][m // 2], 4096, [("scr", "mix", m // 2)])
                    slab = slab.rearrange("p (t c) -> p t c", t=16)
                bk = self.bank()
                for t in range(16):
                    trig, kc = t // 8, t % 8
                    self.mm(self.ps[:, bk, :], slab[:, t, (m % 2) * 128:(m % 2 + 1) * 128], rv(trig, kc), t == 0, t == 15,
                            sbufs + rb(trig, kc), [self.PS(bk)])
                xs = self.XTv[:, m, sb * 512:(sb + 1) * 512]
                self.stt(xs, self.ps[:, bk, :], self.pr("fb", m), xs, ALU.add, ALU.add,
                         [self.PS(bk), self.B("pr", "fb"), self.XT(m, sb)], [self.XT(m, sb)], cost=0.25)

        for z in range(2):
            for trig in range(2):
                u = z * 2 + trig
                if u == 0 and getattr(self, "pre_unit", None) is not None:
                    unit, ubufs, _ = self.pre_unit
                    self.pre_unit = None
                else:
                    unit, ubufs, _ = self.dfs.pop(dt["dftu"][u], 8192, [])
                unit = unit.rearrange("p (k j) -> p k j", k=16)
                for c in range(8):
                    bk = self.bank()
                    for kc in range(16):
                        xnb = [self.XN(kc // 2, 2 * (kc % 2)), self.XN(kc // 2, 2 * (kc % 2) + 1)]
                        self.mm(self.ps[:, bk, :], self.XNT[:, kc, c * 128:(c + 1) * 128], unit[:, kc, :], kc == 0, kc == 15,
                                xnb + ubufs, [self.PS(bk)])
                    eng = self.pick(["act", "dve"], 0.25)
                    self.cp(eng, Zv(z, trig, c), self.ps[:, bk, :], [self.PS(bk)], Zb(z, trig, c))
            if z == 1:
                bk = self.bank()
                for c in range(8):
                    for kc in range(16):
                        xnb = [self.XN(kc // 2, 2 * (kc % 2)), self.XN(kc // 2, 2 * (kc % 2) + 1)]
                        self.mm(self.ps[:, bk, c:c + 1], self.XNT[:, kc, c * 128:(c + 1) * 128], DFTN[:, kc:kc + 1], kc == 0, kc == 15,
                                xnb + dftnb, [self.PS(bk)])
                self.cp("dve", NYQ, self.ps[:, bk, 0:8], [self.PS(bk)], nyqb, cost=0.05)
            mix(z, lambda trig, kc, z=z: Zv(z, trig, kc), lambda trig, kc, z=z: Zb(z, trig, kc))
        for zz in range(2):
            for trig in range(2):
                sgn = 1.0 if trig == 0 else -1.0
                for c in range(8):
                    mv, mb = Mv(zz, trig, c), Mb(zz, trig, c)
                    eng = self.pick(["act", "dve"], 0.25)
                    if zz == 0:
                        if trig == 0:
                            self.cp(eng, mv[:, 0:1], NYQ[:, c:c + 1], nyqb, mb, cost=0.0)
                        else:
                            self.memset("pool", mv[:, 0:1], 0.0, mb)
                        src, srcb = Zv(1, trig, c), Zb(1, trig, c)
                    else:
                        self.cp(eng, mv[:, 0:1], Zv(1, trig, c)[:, 0:1], Zb(1, trig, c), mb, scale=(None if trig == 0 else -1.0), cost=0.0)
                        src, srcb = Zv(0, trig, c), Zb(0, trig, c)
                    self.cp(eng, mv[:, 1:512], src[:, 511:0:-1], srcb, mb, scale=(None if trig == 0 else -1.0), cost=0.0)
            mix(2 + zz, lambda trig, kc, zz=zz: Mv(zz, trig, kc), lambda trig, kc, zz=zz: Mb(zz, trig, kc))

    def fnorm(self, gname, inplace=False):
        q = self.quad()
        for kc in range(8):
            k2 = kc % 2
            sq = self.a16(56 + 4 * k2, 2048)
            sqb = self.AR(56 + 4 * k2, 60 + 4 * k2)
            eng = "act" if kc % 2 == 0 else "dve"
            if eng == "act":
                self.act(sq, self.XTv[:, kc, :], AF.Square, self.XTr(kc), sqb, cost=0.0)
            else:
                self.tt("dve", sq, self.XTv[:, kc, :], self.XTv[:, kc, :], ALU.mult, self.XTr(kc), sqb, cost=0.0)
            for tb in range(4):
                self.mm(self.ps[:, q + tb, :], self.ONESN, sq[:, tb * 512:(tb + 1) * 512], kc == 0, kc == 7,
                        sqb + [self.B("onesn")], [self.PS(q + tb)])
        RS = self.a32(48, 2048)
        rsb = self.AR(48, 56)
        self.act(RS, self.psq(q), AF.Ln, self.PSq(q), rsb, bias=EPS, scale=1.0)
        self.act(RS, RS, AF.Exp, rsb, rsb, scale=-0.5)
        for kc in range(8):
            if inplace:
                self.stt(self.XTv[:, kc, :], self.XTv[:, kc, :], self.pr(gname, kc), RS, ALU.mult, ALU.mult,
                         self.XTr(kc) + rsb + [self.B("pr", gname)], self.XTr(kc))
            else:
                self.stt(self.XNF[:, kc, :], self.XTv[:, kc, :], self.pr(gname, kc), RS, ALU.mult, ALU.mult,
                         self.XTr(kc) + rsb + [self.B("pr", gname)], self.XNr(kc))

    def xa_views(self):
        QT = self.a16(0, 16384).rearrange("p (k s) -> p k s", k=8)
        KT = self.a16(32, 2048).rearrange("p (k m) -> p k m", k=8)
        V = self.a16(36, 2048).rearrange("p (c n) -> p c n", c=2)
        MNT = self.a16(64, 2048).rearrange("p (k m) -> p k m", k=8)
        return QT, KT, self.AR(32, 36), V, self.AR(36, 40), MNT, self.AR(64, 68)

    def xa_kv(self, l, seq):
        dt = self.dt
        B = self.B
        QT, KT, ktb, V, vb, MNT, mntb = self.xa_views()
        for mt in range(2):
            mstg = self.a32(68 + 4 * mt, 1024)
            msb = self.AR(68 + 4 * mt, 72 + 4 * mt)
            self.dma(mstg, dt["mem"][seq, mt * 128:(mt + 1) * 128, :], [], msb)
            css, cssb = self.col()
            crs, crsb = self.col()
            junk = self.a16(76, 1024)
            self.act(junk, mstg, AF.Square, msb, self.AR(76, 78) + cssb, accum_out=css, cost=0.5)
            self.act(crs, css, AF.Sqrt, cssb, crsb, bias=EPS, scale=1.0 / D, cost=0.05)
            self.recip(crs, crs, crsb, crsb, cost=0.05)
            self.act(mstg, mstg, AF.Copy, msb + crsb, msb, scale=crs, cost=0.5)
            for half in range(2):
                bk = self.bank()
                for cc in range(4):
                    c = half * 4 + cc
                    self.tr(self.ps[:, bk, cc * 128:(cc + 1) * 128], mstg[:, c * 128:(c + 1) * 128], msb, [self.PS(bk)])
                for cc in range(4):
                    c = half * 4 + cc
                    self.ts("dve", MNT[:, c, mt * 128:(mt + 1) * 128], self.ps[:, bk, cc * 128:(cc + 1) * 128],
                            self.pr("nmem%d" % l, c), None, ALU.mult, None, [self.PS(bk), B("pr", "nmem%d" % l)], mntb, cost=0.1)
        for hc in range(8):
            if hc % 4 == 0:
                slab, sbufs, _ = self.ws.pop(self.scr["kv%d" % l][hc // 4], 4096, [("scr", "kv%d" % l, hc // 4)])
                slab = slab.rearrange("p (k n) -> p k n", k=8)
            bk = self.bank()
            for kc in range(8):
                self.mm(self.ps[:, bk, 0:256], slab[:, kc, (hc % 4) * 128:(hc % 4 + 1) * 128], MNT[:, kc, :], kc == 0, kc == 7,
                        sbufs + mntb, [self.PS(bk)])
            eng = self.pick(["act", "dve"], 0.15)
            self.cp(eng, KT[:, hc, :], self.ps[:, bk, 0:256], [self.PS(bk)], ktb, cost=0.0)
        for nb in range(2):
            slab, sbufs, _ = self.ws.pop(self.scr["kv%d" % l][2 + nb], 4096, [("scr", "kv%d" % l, 2 + nb)])
            slab = slab.rearrange("p (k n) -> p k n", k=8)
            for mc in range(2):
                bk = self.bank()
                for kc in range(8):
                    self.mm(self.ps[:, bk, :], MNT[:, kc, mc * 128:(mc + 1) * 128], slab[:, kc, :], kc == 0, kc == 7,
                            sbufs + mntb, [self.PS(bk)])
                eng = self.pick(["act", "dve"], 0.25)
                self.cp(eng, V[:, mc, nb * 512:(nb + 1) * 512], self.ps[:, bk, :], [self.PS(bk)], vb, cost=0.0)

    def xa(self, l, seq):
        B = self.B
        QT, KT, ktb, V, vb, MNT, mntb = self.xa_views()
        qtb = lambda m, tb: self.AR(m * 4 + tb, m * 4 + tb + 1)
        for m in range(8):
            if m % 4 == 0:
                slab, sbufs, _ = self.ws.pop(self.scr["q%d" % l][m // 4], 4096, [("scr", "q%d" % l, m // 4)])
                slab = slab.rearrange("p (k n) -> p k n", k=8)
            q = self.quad()
            for kc in range(8):
                for tb in range(4):
                    self.mm(self.ps[:, q + tb, :], slab[:, kc, (m % 4) * 128:(m % 4 + 1) * 128], self.XNF[:, kc, tb * 512:(tb + 1) * 512],
                            kc == 0, kc == 7, sbufs + [self.XN(kc, tb)], [self.PS(q + tb)])
            eng = self.pick(["act", "dve"], 1.0)
            self.cp(eng, QT[:, m, :], self.psq(q), self.PSq(q), self.AR(m * 4, m * 4 + 4), scale=0.0625, cost=0.0)
        it = 0
        for h in range(4):
            for tb in range(4):
                par = it % 2
                it += 1
                PT = self.a16(40 + 2 * par, 1024).rearrange("p (c s) -> p c s", c=2)
                ptb = self.AR(40 + 2 * par, 42 + 2 * par)
                RSv = self.a32(44 + 2 * par, 512)
                rsb = self.AR(44 + 2 * par, 46 + 2 * par)
                for mc in range(2):
                    bk = self.bank()
                    for dc in range(2):
                        self.mm(self.ps[:, bk, :], KT[:, h * 2 + dc, mc * 128:(mc + 1) * 128], QT[:, h * 2 + dc, tb * 512:(tb + 1) * 512],
                                dc == 0, dc == 1, ktb + qtb(h * 2 + dc, tb), [self.PS(bk)])
                    self.act(PT[:, mc, :], self.ps[:, bk, :], AF.Exp, [self.PS(bk)], ptb, cost=0.25)
                bs = self.bank()
                for mc in range(2):
                    self.mm(self.ps[:, bs, :], self.ONES1, PT[:, mc, :], mc == 0, mc == 1, ptb + [B("ones1")], [self.PS(bs)])
                self.recip(RSv, self.ps[:, bs, :], [self.PS(bs)], rsb, cost=0.25)
                for dc in range(2):
                    bk = self.bank()
                    for mc in range(2):
                        self.mm(self.ps[:, bk, :], V[:, mc, (h * 2 + dc) * 128:(h * 2 + dc + 1) * 128], PT[:, mc, :], mc == 0, mc == 1,
                                vb + ptb, [self.PS(bk)])
                    self.tt("dve", self.XNF[:, h * 2 + dc, tb * 512:(tb + 1) * 512], self.ps[:, bk, :], RSv, ALU.mult,
                            [self.PS(bk)] + rsb, [self.XN(h * 2 + dc, tb)], cost=0.25)
        for m in range(8):
            if m % 4 == 0:
                slab, sbufs, _ = self.ws.pop(self.scr["o%d" % l][m // 4], 4096, [("scr", "o%d" % l, m // 4)])
                slab = slab.rearrange("p (k n) -> p k n", k=8)
            q = self.quad()
            for kc in range(8):
                for tb in range(4):
                    self.mm(self.ps[:, q + tb, :], slab[:, kc, (m % 4) * 128:(m % 4 + 1) * 128], self.XNF[:, kc, tb * 512:(tb + 1) * 512],
                            kc == 0, kc == 7, sbufs + [self.XN(kc, tb)], [self.PS(q + tb)])
            self.tt("dve", self.XTv[:, m, :], self.psq(q), self.XTv[:, m, :], ALU.add, self.PSq(q) + self.XTr(m), self.XTr(m))

    def ffn(self, l):
        B = self.B
        GBUF = self.a32(44, 2050)
        gbb = self.AR(44, 53)
        C = self.a32(53, 2048)
        cb = self.AR(53, 61)
        self.memset("pool", GBUF[:, 0:1], 0.0, gbb)
        self.memset("pool", GBUF[:, 2049:2050], 0.0, gbb)
        pre = "f%d" % l
        for gi, (g0, g1) in enumerate(FFN_GROUPS):
            gl = g1 - g0
            for j in range(g0, g1):
                slab, sbufs, _ = self.ws.pop(self.scr["up%d" % l][j], 2048, [("scr", "up%d" % l, j)])
                slab = slab[:, 0:2048].rearrange("p (g k c) -> p g k c", g=2, k=8)
                qg = self.quad()
                for kc in range(8):
                    for tb in range(4):
                        self.mm(self.ps[:, qg + tb, :], slab[:, 0, kc, :], self.XNF[:, kc, tb * 512:(tb + 1) * 512], kc == 0, kc == 7,
                                sbufs + [self.XN(kc, tb)], [self.PS(qg + tb)])
                self.act(GBUF[:, 1:2049], self.psq(qg), AF.Copy, self.PSq(qg), gbb)
                qv = self.quad()
                for kc in range(8):
                    for tb in range(4):
                        self.mm(self.ps[:, qv + tb, :], slab[:, 1, kc, :], self.XNF[:, kc, tb * 512:(tb + 1) * 512], kc == 0, kc == 7,
                                sbufs + [self.XN(kc, tb)], [self.PS(qv + tb)])
                self.ts("dve", C, GBUF[:, 0:2048], self.pr(pre + "cw0", j), self.pr(pre + "cb", j), ALU.mult, ALU.add,
                        gbb + [B("pr", pre + "cw0"), B("pr", pre + "cb")], cb)
                self.stt(C, GBUF[:, 1:2049], self.pr(pre + "cw1", j), C, ALU.mult, ALU.add, gbb + cb + [B("pr", pre + "cw1")], cb)
                self.stt(C, GBUF[:, 2:2050], self.pr(pre + "cw2", j), C, ALU.mult, ALU.add, gbb + cb + [B("pr", pre + "cw2")], cb)
                self.act(C, C, AF.Gelu_apprx_tanh, cb, cb)
                jj = j - g0
                self.tt("dve", self.a16(jj * 4, 2048), self.psq(qv), C, ALU.mult, self.PSq(qv) + cb, self.AR(jj * 4, jj * 4 + 4))
            for m in range(8):
                if m % 2 == 0:
                    ne = 2 * gl * 128
                    slab, sbufs, _ = self.ws.pop(self.scr["dn%d" % l][gi * 4 + m // 2][:, 0:ne], ne, [("scr", "dn%d" % l, gi * 4 + m // 2)])
                    slab = slab[:, 0:ne].rearrange("p (m k c) -> p m k c", m=2, k=gl)
                q = self.quad()
                for kk in range(gl):
                    hv = self.a16(kk * 4, 2048)
                    for tb in range(4):
                        self.mm(self.ps[:, q + tb, :], slab[:, m % 2, kk, :], hv[:, tb * 512:(tb + 1) * 512], kk == 0, kk == gl - 1,
                                sbufs + self.AR(kk * 4 + tb, kk * 4 + tb + 1), [self.PS(q + tb)])
                self.tt("dve", self.XTv[:, m, :], self.psq(q), self.XTv[:, m, :], ALU.add, self.PSq(q) + self.XTr(m), self.XTr(m))

    def lru(self):
        B = self.B
        REC = self.a32(0, 2051)
        recb = self.AR(0, 9)
        C = self.a32(9, 2048); cb = self.AR(9, 17)
        C16 = self.a16(17, 2048); c16b = self.AR(17, 21)
        X = [self.a32(21, 2048), self.a32(29, 2048)]
        xb = [self.AR(21, 29), self.AR(29, 37)]
        A = [self.a32(37, 2048), self.a32(45, 2048)]
        ab = [self.AR(37, 45), self.AR(45, 53)]
        T = [self.a32(53, 2048), self.a32(61, 2048)]
        tb_ = [self.AR(53, 61), self.AR(61, 69)]
        HG = lambda jj: self.a16(69 + 4 * jj, 2048)
        hgb = lambda jj: self.AR(69 + 4 * jj, 73 + 4 * jj)
        self.memset("pool", REC[:, 0:2], 0.0, recb)
        self.memset("pool", REC[:, 2050:2051], 0.0, recb)
        prb = [B("pr", n_) for n_ in ("lcw0", "lcw1", "lcw2", "lcw3", "lcb", "lba0", "lba1", "lbi0", "lbi1", "lk0", "lk1")]
        st = {}

        QA, QB = 0, 4

        def front_a(j):
            slab, sbufs, idx = self.ws.pop(self.scr["lin"][j], 2560, [("scr", "lin", j)], hold=True)
            st[j] = (slab, sbufs, idx)
            wrec = slab[:, 0:1024].rearrange("p (k c) -> p k c", k=8)
            q = QB
            for kc in range(8):
                for tb in range(4):
                    self.mm(self.ps[:, q + tb, :], wrec[:, kc, :], self.XNF[:, kc, tb * 512:(tb + 1) * 512], kc == 0, kc == 7,
                            sbufs + [self.XN(kc, tb)], [self.PS(q + tb)])
            self.act(REC[:, 2:2050], self.psq(q), AF.Copy, self.PSq(q), recb)
            self.act(C, REC[:, 0:2048], AF.Identity, recb + prb, cb, scale=self.pr("lcw0", j), bias=self.pr("lcb", j))
            for k in range(1, 4):
                self.stt(C, REC[:, k:k + 2048], self.pr("lcw%d" % k, j), C, ALU.mult, ALU.add, recb + cb + prb, cb)

        def front_b(j):
            self.act(C16, C, AF.Copy, cb, c16b)

        def gates(j, t, q):
            slab, sbufs, idx = st[j]
            w = slab[:, 2048 + t * 128:2048 + (t + 1) * 128]
            for tb in range(4):
                self.mm(self.ps[:, q + tb, :], w, C16[:, tb * 512:(tb + 1) * 512], True, True, sbufs + c16b, [self.PS(q + tb)])
            return q

        def mid_a(j):
            slab, sbufs, idx = st[j]
            wgate = slab[:, 1024:2048].rearrange("p (k c) -> p k c", k=8)
            qq = [gates(j, 0, QA), gates(j, 1, QB)]
            for e in range(2):
                qa = qq[e]
                self.act(A[e], self.psq(qa), AF.Tanh, self.PSq(qa) + prb, ab[e], bias=self.pr("lba%d" % e, j), scale=0.5)
                self.act(A[e], A[e], AF.Exp, ab[e] + prb, ab[e], scale=self.pr("lk%d" % e, j), bias=self.pr("lk%d" % e, j))
                self.act(T[e], A[e], AF.Square, ab[e], tb_[e])
                if e == 0:
                    qg = QA
                    for kc in range(8):
                        for tb in range(4):
                            self.mm(self.ps[:, qg + tb, :], wgate[:, kc, :], self.XNF[:, kc, tb * 512:(tb + 1) * 512], kc == 0, kc == 7,
                                    sbufs + [self.XN(kc, tb)], [self.PS(qg + tb)])
            for e in range(2):
                qi = gates(j, 2 + e, QB)
                self.act(X[e], self.psq(qi), AF.Tanh, self.PSq(qi) + prb, xb[e], bias=self.pr("lbi%d" % e, j), scale=0.5)
            self.ws.release(idx)
            for e in range(2):
                self.stt(X[e], X[e], 1.0, C, ALU.add, ALU.mult, xb[e] + cb, xb[e])
            return qg

        def mid_b(j, qg):
            for e in range(2):
                self.act(T[e], T[e], AF.Sqrt, tb_[e], tb_[e], bias=0.25, scale=-0.25)
            for e in range(2):
                self.tt("dve", T[e], T[e], X[e], ALU.mult, tb_[e] + xb[e], tb_[e], cost=1.0)
            self.act(X[0], self.psq(qg), AF.Gelu_apprx_tanh, self.PSq(qg), xb[0])
            T0, A0, T1, A1 = T[0], A[0], T[1], A[1]
            self.P.op("dve", lambda en: en.tensor_tensor_scan(out=T0, data0=A0, data1=T0, initial=0.0, op0=ALU.mult, op1=ALU.add),
                      ab[0] + tb_[0], tb_[0])
            self.P.op("dve", lambda en: en.tensor_tensor_scan(out=T1[:, ::-1], data0=A1[:, ::-1], data1=T1[:, ::-1], initial=0.0,
                                                              op0=ALU.mult, op1=ALU.add), ab[1] + tb_[1], tb_[1])
            self.load["dve"] += 4.0

        def wout(g):
            slab, sbufs, idx = self.ws.pop(self.scr["lout"][g], 2048, [("scr", "lout", g)], hold=True)
            slab = slab[:, 0:2048].rearrange("p (k n) -> p k n", k=2)
            for m in range(8):
                q = QB if m % 2 == 0 else QA
                for kk in range(2):
                    hv = HG(kk)
                    for tb in range(4):
                        self.mm(self.ps[:, q + tb, :], slab[:, kk, m * 128:(m + 1) * 128], hv[:, tb * 512:(tb + 1) * 512], kk == 0, kk == 1,
                                sbufs + self.AR(69 + 4 * kk + tb, 70 + 4 * kk + tb), [self.PS(q + tb)])
                self.tt("dve", self.XTv[:, m, :], self.psq(q), self.XTv[:, m, :], ALU.add, self.PSq(q) + self.XTr(m), self.XTr(m))
            self.ws.release(idx)

        front_a(0)
        front_b(0)
        for j in range(NJ):
            qg = mid_a(j)
            mid_b(j, qg)
            if j + 1 < NJ:
                front_a(j + 1)
            self.tt("dve", T[0], T[0], T[1], ALU.add, tb_[0] + tb_[1], tb_[0], cost=1.0)
            if j + 1 < NJ:
                front_b(j + 1)
            self.tt("dve", HG(j % 2), T[0], X[0], ALU.mult, tb_[0] + xb[0], hgb(j % 2))
            if j % 2 == 1:
                wout(j // 2)
        self.quad_i = 0

    def final(self, seq):
        self.fnorm("nfin", inplace=True)
        if seq + 1 < self.nseq and not self.ws.recording:
            dt = self.dt
            for i in range(4):
                stg = self.a32(48 + 4 * i, 1024)
                self.dma(stg, dt["x"][seq + 1, i * 128:(i + 1) * 128, :], [], self.AR(48 + 4 * i, 52 + 4 * i))
            self.x_pre = seq + 1
            self.pre_unit = self.dfs.pop(dt["dftu"][0], 8192, [])
        for i in range(16):
            k = i % 2
            ostg = self.a32(68 + 4 * k, 1024)
            ob = self.AR(68 + 4 * k, 72 + 4 * k)
            for half in range(2):
                bk = self.bank()
                for cc in range(4):
                    c = half * 4 + cc
                    self.tr(self.ps[:, bk, cc * 128:(cc + 1) * 128], self.XTv[:, c, i * 128:(i + 1) * 128], [self.XT(c, i // 4)], [self.PS(bk)])
                eng = "act" if half == 0 else "dve"
                self.cp(eng, ostg[:, half * 512:(half + 1) * 512], self.ps[:, bk, :], [self.PS(bk)], ob, cost=0.0)
            self.dma(self.y[seq, i * 128:(i + 1) * 128, :], ostg, ob, [self.B("y", seq, i)])


_CONST = {}


def _consts():
    if _CONST:
        return _CONST
    n = np.arange(S, dtype=np.int64)
    ang = ((n[:, None] * n[None, :]) % S).astype(np.float64) * (2.0 * np.pi / S)
    sc = 1.0 / np.sqrt(S)
    tabs = [np.cos(ang) * sc, np.sin(ang) * sc]
    units = np.empty((8, 128, 8192), dtype=ml_dtypes.bfloat16)
    for sb in range(4):
        for trig in range(2):
            t = tabs[trig][:, sb * 512:(sb + 1) * 512]
            t = t.reshape(16, 128, 512).transpose(1, 0, 2).reshape(128, 8192)
            units[sb * 2 + trig] = t.astype(ml_dtypes.bfloat16)
    m = np.arange(256, dtype=np.int64)
    ang2 = ((m[:, None] * m[None, :]) % 256).astype(np.float64) * (2.0 * np.pi / 256)
    cd = np.stack([np.cos(ang2) / 16.0, -np.sin(ang2) / 16.0]).astype(np.float32)
    nq = (np.where(n % 2 == 0, 1.0, -1.0) * sc).reshape(16, 128).T
    _CONST["dftn"] = np.ascontiguousarray(nq).astype(ml_dtypes.bfloat16)
    _CONST["dftu"] = units
    _CONST["cdft"] = cd
    _CONST["ident"] = np.eye(128, dtype=np.float32)
    return _CONST


_W_NAMES = ["norm_mix", "fnet_w_out", "fnet_b_out", "lru_w_in", "lru_conv_w", "lru_conv_b", "lru_w_a", "lru_b_a",
            "lru_w_i", "lru_b_i", "lru_lambda", "lru_w_out", "norm_xa", "norm_mem", "xa_w_q", "xa_w_kv", "xa_w_o",
            "norm_ffn", "ffn_w_up", "ffn_conv_w", "ffn_conv_b", "ffn_w_down", "norm_final"]

_NC_CACHE = {}


def kernel(**inputs):
    c = _consts()
    xp = np.asarray(inputs["x_prompt"], dtype=np.float32)
    xs = np.asarray(inputs["x_sample"], dtype=np.float32)
    mp = np.asarray(inputs["mem_prompt"], dtype=np.float32)
    ms = np.asarray(inputs["mem_sample"], dtype=np.float32)
    if "nc" not in _NC_CACHE:
        b = Builder(nseq=6)
        _NC_CACHE["nc"] = b.build()
    nc = _NC_CACHE["nc"]
    in_maps = []
    for i in range(NCORES):
        d = {"x": np.ascontiguousarray(np.concatenate([xp[2 * i:2 * i + 2], xs[4 * i:4 * i + 4]], axis=0)),
             "mem": np.ascontiguousarray(np.concatenate([mp[2 * i:2 * i + 2], ms[4 * i:4 * i + 4]], axis=0)),
             "dftu": c["dftu"], "cdft": c["cdft"], "ident": c["ident"], "dftn": c["dftn"]}
        for nm in _W_NAMES:
            d[nm] = np.ascontiguousarray(np.asarray(inputs[nm], dtype=np.float32))
        in_maps.append(d)
    res = run_bass_kernel_spmd(nc, in_maps, core_ids=list(range(NCORES)))
    ys = [np.asarray(r["y"]) for r in res.results]
    y_prompt = np.concatenate([y[0:2] for y in ys], axis=0).astype(np.float32)
    y_sample = np.concatenate([y[2:6] for y in ys], axis=0).astype(np.float32)
    return (y_prompt, y_sample)
```
